# Optimizing a Trainium2 kernel written in Bass

```python
import jax, jax.numpy as jnp
from jax import lax
import numpy as np

D_MODEL = 2048
BATCH = 2
SEQ = 4096
DEPTH = 4
DEC_BATCH = 32
DEC_SEQ = 4
PAST_LEN = 16384
PAGE_SIZE = 128

N_META = 16
MIX_DIM = D_MODEL
ATT_DIM = MIX_DIM // 2
CONV_DIM = MIX_DIM - ATT_DIM
HEAD_DIM = 64
N_HEADS = ATT_DIM // HEAD_DIM
N_KV_HEADS = 4
GQA_GROUP = N_HEADS // N_KV_HEADS
KV_DIM = N_KV_HEADS * HEAD_DIM
WINDOW = 128
BLOCK = 128
CONV_WIDTH = 3
PROJ_DIM = ATT_DIM + 2 * KV_DIM + ATT_DIM + 4 * CONV_DIM
DEEPNORM_ALPHA = (2 * DEPTH) ** 0.25
DEEPNORM_BETA = (8 * DEPTH) ** -0.25
LN_EPS = 1e-5
NEG_INF = -1e30

kernel_name = "hymba_swa_sink_gated_conv_deepnorm_step"


def alibi_slopes():
    return 2.0 ** (-8.0 * jnp.arange(1, N_HEADS + 1, dtype=jnp.float32) / N_HEADS)


def layer_norm(x, g, b):
    xf = x.astype(jnp.float32)
    mu = xf.mean(-1, keepdims=True)
    var = jnp.square(xf - mu).mean(-1, keepdims=True)
    y = (xf - mu) * lax.rsqrt(var + LN_EPS) * g.astype(jnp.float32) + b.astype(jnp.float32)
    return y.astype(x.dtype)


def split_projection(x, w_in):
    n, t = x.shape[:2]
    h = jnp.einsum('btd,dp->btp', x, w_in)
    splits = [ATT_DIM, ATT_DIM + KV_DIM, ATT_DIM + 2 * KV_DIM, 2 * ATT_DIM + 2 * KV_DIM,
              2 * ATT_DIM + 2 * KV_DIM + CONV_DIM, 2 * ATT_DIM + 2 * KV_DIM + 2 * CONV_DIM,
              2 * ATT_DIM + 2 * KV_DIM + 3 * CONV_DIM]
    q, k, v, g_att, b_gate, c_gate, h_conv, g_conv = jnp.split(h, splits, axis=-1)
    q = q.reshape(n, t, N_KV_HEADS, GQA_GROUP, HEAD_DIM)
    k = k.reshape(n, t, N_KV_HEADS, HEAD_DIM)
    v = v.reshape(n, t, N_KV_HEADS, HEAD_DIM)
    u = c_gate * h_conv
    return q, k, v, g_att, b_gate, u, g_conv


def sink_attention(q, k, v, dist, mask, sinks):
    scale = HEAD_DIM ** -0.5
    s = jnp.einsum('...qkgd,...skd->...kgqs', q, k, preferred_element_type=jnp.float32) * scale
    slopes = alibi_slopes().reshape(N_KV_HEADS, GQA_GROUP, 1, 1)
    s = jnp.where(mask, s - slopes * dist.astype(jnp.float32), NEG_INF)
    sink = sinks.astype(jnp.float32).reshape(N_KV_HEADS, GQA_GROUP, 1, 1)
    m = jnp.maximum(s.max(-1, keepdims=True), sink)
    p = jnp.exp(s - m)
    denom = p.sum(-1, keepdims=True) + jnp.exp(sink - m)
    p = (p / denom).astype(v.dtype)
    return jnp.einsum('...kgqs,...skd->...qkgd', p, v)


def prompt_window_attention(q, k, v, sinks):
    b, L = q.shape[:2]
    pad = (-L) % BLOCK
    nb = (L + pad) // BLOCK
    qp = jnp.pad(q, ((0, 0), (pad, 0), (0, 0), (0, 0), (0, 0))).reshape(b, nb, BLOCK, N_KV_HEADS, GQA_GROUP, HEAD_DIM)

    def banded(t):
        tb = jnp.pad(t, ((0, 0), (pad, 0), (0, 0), (0, 0))).reshape(b, nb, BLOCK, N_KV_HEADS, HEAD_DIM)
        prev = jnp.pad(tb[:, :-1], ((0, 0), (1, 0), (0, 0), (0, 0), (0, 0)))
        return jnp.concatenate([prev, tb], axis=2)

    kk, vv = banded(k), banded(v)
    i = jnp.arange(BLOCK)[:, None]
    r = jnp.arange(2 * BLOCK)[None, :]
    dist = BLOCK + i - r
    k_pos = (jnp.arange(nb)[:, None, None] - 1) * BLOCK + r[None] - pad
    mask = (dist >= 0) & (dist < WINDOW) & (k_pos >= 0)
    mask = mask[:, None, None]
    o = sink_attention(qp, kk, vv, dist, mask, sinks)
    return o.reshape(b, nb * BLOCK, ATT_DIM)[:, pad:]


def sample_window_attention(q, k_new, v_new, cache_k, cache_v, sinks):
    n, t = q.shape[:2]
    w = cache_k.shape[1]
    kk = jnp.concatenate([cache_k.astype(k_new.dtype), k_new], axis=1)
    vv = jnp.concatenate([cache_v.astype(v_new.dtype), v_new], axis=1)
    i = jnp.arange(t)[:, None]
    r = jnp.arange(w + t)[None, :]
    dist = w + i - r
    mask = (dist >= 0) & (dist < WINDOW)
    o = sink_attention(q, kk, vv, dist, mask, sinks)
    return o.reshape(n, t, ATT_DIM), kk[:, -WINDOW:], vv[:, -WINDOW:]


def causal_conv(u, w, prev):
    t = u.shape[1]
    ext = jnp.concatenate([prev.astype(u.dtype), u], axis=1)
    y = w[0] * ext[:, 0:t]
    for j in range(1, CONV_WIDTH):
        y = y + w[j] * ext[:, j:j + t]
    return y, ext[:, -(CONV_WIDTH - 1):]


def combine(x, o_att, o_conv, g_att, g_conv, w_out, ln_g, ln_b):
    mix = jnp.concatenate([o_att * jax.nn.silu(g_att), o_conv * jax.nn.silu(g_conv)], axis=-1)
    out = jnp.einsum('btm,md->btd', mix, w_out)
    return layer_norm(DEEPNORM_ALPHA * x + out, ln_g, ln_b)


def setup_inputs(seed: int = 0) -> dict:
    key = jax.random.key(seed)
    ks = jax.random.split(key, 12)
    f32 = jnp.float32
    return {
        "x_prompt": jax.random.normal(ks[0], (BATCH, SEQ, D_MODEL), f32),
        "x_sample": jax.random.normal(ks[1], (DEC_BATCH, DEC_SEQ, D_MODEL), f32),
        "cache_k": jax.random.normal(ks[2], (DEPTH, DEC_BATCH, WINDOW, N_KV_HEADS, HEAD_DIM), f32),
        "cache_v": jax.random.normal(ks[3], (DEPTH, DEC_BATCH, WINDOW, N_KV_HEADS, HEAD_DIM), f32),
        "state_conv": jax.random.normal(ks[4], (DEPTH, DEC_BATCH, CONV_WIDTH - 1, CONV_DIM), f32),
        "meta_tokens": jax.random.normal(ks[5], (N_META, D_MODEL), f32),
        "w_in": jax.random.normal(ks[6], (DEPTH, D_MODEL, PROJ_DIM), f32) * D_MODEL ** -0.5,
        "conv_w": jax.random.normal(ks[7], (DEPTH, CONV_WIDTH, CONV_DIM), f32) * CONV_WIDTH ** -0.5,
        "sinks": jax.random.normal(ks[8], (DEPTH, N_HEADS), f32) * 0.5,
        "w_out": jax.random.normal(ks[9], (DEPTH, MIX_DIM, D_MODEL), f32) * (MIX_DIM ** -0.5 * DEEPNORM_BETA),
        "ln_g": 1.0 + 0.02 * jax.random.normal(ks[10], (DEPTH, D_MODEL), f32),
        "ln_b": 0.02 * jax.random.normal(ks[11], (DEPTH, D_MODEL), f32),
    }


def reference(x_prompt, x_sample, cache_k, cache_v, state_conv, meta_tokens,
              w_in, conv_w, sinks, w_out, ln_g, ln_b):
    b = x_prompt.shape[0]
    meta = jnp.broadcast_to(meta_tokens[None].astype(x_prompt.dtype), (b, N_META, D_MODEL))
    xp = jnp.concatenate([meta, x_prompt], axis=1)
    xs = x_sample
    kp_l, vp_l, cp_l, ks_l, vs_l, cs_l = [], [], [], [], [], []
    for l in range(DEPTH):
        q, k, v, g_a, b_g, u, g_c = split_projection(xp, w_in[l])
        o_att = prompt_window_attention(q, k, v, sinks[l])
        cy, c_state = causal_conv(u, conv_w[l], jnp.zeros((b, CONV_WIDTH - 1, CONV_DIM), u.dtype))
        xp = combine(xp, o_att, b_g * cy, g_a, g_c, w_out[l], ln_g[l], ln_b[l])
        kp_l.append(k[:, -WINDOW:])
        vp_l.append(v[:, -WINDOW:])
        cp_l.append(c_state)
        q, k, v, g_a, b_g, u, g_c = split_projection(xs, w_in[l])
        o_att, k_buf, v_buf = sample_window_attention(q, k, v, cache_k[l], cache_v[l], sinks[l])
        cy, c_state = causal_conv(u, conv_w[l], state_conv[l])
        xs = combine(xs, o_att, b_g * cy, g_a, g_c, w_out[l], ln_g[l], ln_b[l])
        ks_l.append(k_buf)
        vs_l.append(v_buf)
        cs_l.append(c_state)
    y_prompt = xp[:, N_META:]
    return (y_prompt, xs, jnp.stack(kp_l), jnp.stack(vp_l), jnp.stack(cp_l),
            jnp.stack(ks_l), jnp.stack(vs_l), jnp.stack(cs_l))
```

```python
import os
import numpy as np
from contextlib import ExitStack
import concourse.bass as bass
import concourse.mybir as mybir
from concourse.bass_utils import run_bass_kernel_spmd

F32 = mybir.dt.float32
BF16 = mybir.dt.bfloat16
AF = mybir.ActivationFunctionType
ALU = mybir.AluOpType
AX = mybir.AxisListType

D = 2048
DEPTH = 4
NPB = 12
NP = NPB * 128
NS = 16
NT = NP + NS
MC = NT - 128
ALPHA = float((2 * DEPTH) ** 0.25)
EPS = 1e-5
NWIN = 52
NEG = -1.0e9


STOP = float(os.environ.get("KSTOP", "99"))


class _Stop(Exception):
    pass


_STOPPED = [False]


def ckpt(level):
    if STOP <= level:
        _STOPPED[0] = True


class Buf:
    __slots__ = ("w", "r")

    def __init__(self):
        self.w = {}
        self.r = {}


class Eng:
    def __init__(self, name):
        self.name = name
        self.sem = name
        self.count = 0
        self.q = []
        self.waited = {}


class FW:
    def __init__(self, sems, n_dma_sems=10):
        it = iter(sems)
        self.semh = {}
        self.engs = {}
        for name in ("pe", "act", "dve", "pool", "sp"):
            self.semh[name] = next(it)
            self.engs[name] = Eng(name)
        self.dma_ring = {}
        for qn in ("sp", "pool", "act"):
            ring = []
            for i in range(n_dma_sems):
                key = "dma_%s_%d" % (qn, i)
                self.semh[key] = next(it)
                ring.append([key, 0])
            self.dma_ring[qn] = [ring, 0]

    def _collect(self, eng, reads, writes):
        need = {}
        own = eng.sem
        own_skip = (eng.name == "pe")
        for b in reads:
            for k, v in b.w.items():
                if k == own and (own_skip or v > eng.count):
                    continue
                if need.get(k, 0) < v:
                    need[k] = v
        for b in writes:
            for k, v in b.w.items():
                if k == own and (own_skip or v > eng.count):
                    continue
                if need.get(k, 0) < v:
                    need[k] = v
            for k, v in b.r.items():
                if k == own:
                    continue
                if need.get(k, 0) < v:
                    need[k] = v
        waits = []
        for k, v in need.items():
            if eng.waited.get(k, 0) < v:
                eng.waited[k] = v
                waits.append((k, v))
        return waits

    def op(self, engname, fn, reads=(), writes=(), signal=True):
        eng = self.engs[engname]
        waits = self._collect(eng, reads, writes)
        if signal:
            eng.count += 1
            tok = (eng.sem, eng.count)
            eng.q.append((waits, fn, (eng.sem, 1)))
        else:
            tok = (eng.sem, eng.count + 1)
            eng.q.append((waits, fn, None))
        for b in reads:
            if b.r.get(tok[0], 0) < tok[1]:
                b.r[tok[0]] = tok[1]
        for b in writes:
            if b.w.get(tok[0], 0) < tok[1]:
                b.w[tok[0]] = tok[1]
        return tok

    def dma(self, qname, fn, reads=(), writes=()):
        eng = self.engs[qname]
        ringinfo = self.dma_ring[qname]
        ring, idx = ringinfo
        slot = ring[idx]
        ringinfo[1] = (idx + 1) % len(ring)
        key, total = slot
        waits = self._collect(eng, reads, writes)
        if total > 0 and eng.waited.get(key, 0) < total:
            eng.waited[key] = total
            waits.append((key, total))
        slot[1] = total + 16
        tok = (key, total + 16)
        eng.q.append((waits, fn, (key, 16)))
        for b in reads:
            if b.r.get(tok[0], 0) < tok[1]:
                b.r[tok[0]] = tok[1]
        for b in writes:
            if b.w.get(tok[0], 0) < tok[1]:
                b.w[tok[0]] = tok[1]
        return tok

    def final_wait(self, engname, bufs):
        eng = self.engs[engname]
        need = {}
        for b in bufs:
            for d in (b.w, b.r):
                for k, v in d.items():
                    if need.get(k, 0) < v:
                        need[k] = v
        waits = [(k, v) for k, v in need.items() if eng.waited.get(k, 0) < v]
        for k, v in waits:
            eng.waited[k] = v
        eng.q.append((waits, None, None))

    def replay(self, block):
        semh = self.semh

        def run(engname, h):
            for waits, fn, inc in self.engs[engname].q:
                for k, v in waits:
                    h.wait_ge(semh[k], v)
                if fn is not None:
                    ins = fn(h)
                    if inc is not None:
                        ins.then_inc(semh[inc[0]], inc[1])

        @block.tensor
        def _(h):
            run("pe", h)

        @block.scalar
        def _(h):
            run("act", h)

        @block.vector
        def _(h):
            run("dve", h)

        @block.gpsimd
        def _(h):
            run("pool", h)

        @block.sync
        def _(h):
            run("sp", h)


def pieces(c0, c1):
    out = []
    s = c0
    while s < c1:
        n = min(512, c1 - s)
        out.append((s, n))
        s += n
    return out


def head_of(c, e):
    kc2, gi = divmod(c, 4)
    return 4 * (2 * kc2 + e) + gi


def slope(h):
    return float(2.0 ** (-8.0 * (h + 1) / 16.0))


def build_program():
    nc = bass.Bass("TRN2", target_bir_lowering=False)

    def din(name, shape):
        return nc.dram_tensor(name, list(shape), F32, kind="ExternalInput").ap()

    def dout(name, shape):
        return nc.dram_tensor(name, list(shape), F32, kind="ExternalOutput").ap()

    xin = din("xin", [NP, D])
    xsd = din("xs", [NS, D])
    ckd = din("ck", [DEPTH, 4, 128, 256])
    cvd = din("cv", [DEPTH, 4, 128, 256])
    scd = din("sc", [DEPTH, 8, 1024])
    wind = din("win", [DEPTH * NWIN, 128, 2048])
    woutd = din("wout", [DEPTH * 16, 128, 2048])
    lngd = din("lng", [128, 64])
    lnbd = din("lnb", [128, 64])
    cwd = din("cw", [128, 96])
    sinkd = din("sinkbc", [128, 64])
    sinkrd = din("sinkrow", [16, 16])
    dmd = din("dm", [128, 5 * 256])
    dmsd = din("dms", [16, 4 * 132])
    cmd = din("cmask", [128, 386])
    identd = din("ident", [128, 128])

    yd = dout("y", [1024, D])
    ysd = dout("ys", [NS, D])
    kwpd = dout("kwp", [DEPTH, 128, 256])
    vwpd = dout("vwp", [DEPTH, 128, 256])
    cvpd = dout("convp", [DEPTH, 2, 1024])
    kwsd = dout("kws", [DEPTH, 4, 128, 256])
    vwsd = dout("vws", [DEPTH, 4, 128, 256])
    cvsd = dout("convs", [DEPTH, 8, 1024])

    with ExitStack() as es:
        def sb(name, shape, dt):
            return es.enter_context(nc.sbuf_tensor("sb_" + name, list(shape), dt))

        hi = sb("hi", [128, 16, NT], BF16)
        lo = sb("lo", [128, 16, MC], BF16)
        mixT = sb("mixT", [128, 16, MC], BF16)
        kT = sb("kT", [128, 2, NT], BF16)
        vv = sb("vv", [128, NPB, 256], BF16)
        wbuf = [sb("wbuf%d" % i, [128, 16, 128], BF16) for i in range(3)]
        qbuf = [sb("qbuf%d" % i, [128, MC], BF16) for i in range(2)]
        SCRW = 1432
        scr = sb("scr", [128, 2 * SCRW + 3 * 512], F32)
        scrA = scr[:, 0:SCRW]
        scrB = scr[:, SCRW:2 * SCRW]
        zr = [scr[:, 2 * SCRW + i * 512: 2 * SCRW + (i + 1) * 512] for i in range(3)]
        mixf = mixT[:, :, :].rearrange("p c m -> p (c m)").bitcast(F32)
        xst = [mixf[:, k * 2048:(k + 1) * 2048] for k in range(5)]
        ostage8 = [mixf[:, k * 512:(k + 1) * 512].rearrange("p (j f) -> p j f", j=4) for k in range(8)]
        scrA16 = scr[:, 0:SCRW].bitcast(BF16)
        MH = scrA16[:, 0:SCRW]
        ML = scrA16[:, SCRW:2 * SCRW]
        Spf = [sb("Sp%d" % i, [128, 536], F32) for i in range(3)]
        Sp = [t[:, 0:514].rearrange("p (e s) -> p e s", e=2) for t in Spf]
        Spb = [t[:, 0:514].bitcast(BF16).rearrange("p (e s) -> p e s", e=2) for t in Spf]
        PTsbf = [sb("PTsb%d" % i, [128, 512], BF16) for i in range(2)]
        PTsb = [t[:, :].rearrange("p (e s q) -> p e s q", e=2, s=2) for t in PTsbf]
        sqb = PTsbf + [sb("sqb2", [128, 512], BF16)]
        st_negm = [sb("negm%d" % i, [128, 2], F32) for i in range(3)]
        st_den = [sb("den%d" % i, [128, 2], F32) for i in range(3)]
        st_rden = [sb("rden%d" % i, [128, 2], F32) for i in range(3)]
        idf = sb("idf", [128, 128], F32)
        ones_bf = sb("ones_bf", [128, 128], BF16)
        identb = sb("identb", [128, 128], BF16)
        nidentb = sb("nidentb", [128, 128], BF16)
        dm = sb("dm", [128, 5, 256], BF16)
        dms = sb("dms", [16, 4, 132], F32)
        sinkbc = sb("sinkbc", [128, 64], F32)
        sinkrow = sb("sinkrow", [16, 16], F32)
        lng = sb("lng", [128, 64], F32)
        lnb = sb("lnb", [128, 64], F32)
        cw = sb("cw", [128, 96], F32)
        cmask = sb("cmask", [128, 386], BF16)
        qs = sb("qs", [128, 4, 2, 16], BF16)
        Ss1 = sb("Ss1", [16, 4, 133], F32)
        s_negm2 = sb("s_negm2", [16, 8], F32)
        s_den2 = sb("s_den2", [16, 8], F32)
        s_rden2 = sb("s_rden2", [16, 8], F32)
        PTs2 = sb("PTs2", [128, 2, 4, 16], BF16)
        PT22 = sb("PT22", [4, 2, 4, 16], BF16)
        vnew_bf_t = sb("vnew_bf", [4, 4, 256], BF16)
        vnew_bf = [vnew_bf_t[:, i, :] for i in range(4)]
        usx = sb("usx", [128, 8, 4, 6], F32)
        ysx = sb("ysx", [128, 4, 4], F32)
        ucol = sb("ucol", [128, 8, 10], F32)

        kst = zr[0][:, 0:256]
        vst = zr[0][:, 256:512]
        ostage = [t[:, 0:512].rearrange("p (j f) -> p j f", j=4) for t in Spf]
        ps = [es.enter_context(nc.psum_tensor("ps%d" % i, [128, 512], F32)) for i in range(8)]
        ps6b = ps[6][:, :].bitcast(BF16)
        sems = [es.enter_context(nc.semaphore("s%d" % i)) for i in range(5 + 30)]
        fw = FW(sems, n_dma_sems=10)
        if os.environ.get('KVERBOSE'):
            print('sbuf remaining', nc.sbuf_bytes_remaining)
        block = es.enter_context(nc.Block())

        b_hi = [Buf() for _ in range(16)]
        b_lo = [Buf() for _ in range(16)]
        b_hip = [[Buf() for _ in range(4)] for _ in range(16)]
        b_lop = [[Buf() for _ in range(4)] for _ in range(16)]
        b_mix = [Buf() for _ in range(16)]
        b_kT = [Buf() for _ in range(2)]
        b_v = [Buf() for _ in range(NPB)]
        b_w = [Buf() for _ in range(3)]
        b_q = [Buf() for _ in range(2)]
        b_scrA = Buf()
        b_scrB = Buf()
        b_zr = [Buf() for _ in range(3)]
        b_Sp = [Buf() for _ in range(3)]
        b_PTsb = [Buf() for _ in range(2)]
        b_sqb = b_PTsb + [Buf()]
        b_st = [Buf() for _ in range(3)]
        b_ps = [Buf() for _ in range(8)]
        b_const = Buf()
        b_vsbf = Buf(); b_ksT = Buf(); b_qs = Buf(); b_Ss = b_Sp[2]
        b_sst = Buf(); b_PTs = Buf(); b_vnew = [Buf() for _ in range(4)]
        b_usx = Buf(); b_ysx = Buf(); b_ucol = Buf()
        b_ost = b_Sp
        b_xst = [Buf() for _ in range(5)]
        b_ost8 = [Buf() for _ in range(8)]
        b_kst = b_zr[0]; b_vst = b_zr[0]
        b_out = Buf()
        b_Ss1 = Buf()
        b_vsbf2 = [Buf(), Buf()]; b_ksT2 = [Buf(), Buf()]; b_sst2 = [Buf(), Buf()]; b_PTs2 = [Buf(), Buf()]

        _STOPPED[0] = False

        def OP(eng, fn, r=(), w=(), sig=True):
            if _STOPPED[0]:
                return None
            return fw.op(eng, fn, reads=r, writes=w, signal=sig)

        def DMA(q, fn, r=(), w=()):
            if _STOPPED[0]:
                return None
            return fw.dma(q, fn, reads=r, writes=w)

        b_idf = Buf()
        DMA("act", lambda h: h.dma_start(out=idf[:], in_=identd[:, :]), w=[b_const, b_idf])
        for (t, d) in ((sinkbc, sinkd), (sinkrow, sinkrd), (lng, lngd), (lnb, lnbd),
                       (cw, cwd)):
            DMA("act", lambda h, t=t, d=d: h.dma_start(out=t[:], in_=d[:, :]), w=[b_const])
        DMA("pool", lambda h: h.dma_start(out=dm[:], in_=dmd.rearrange("p (t s) -> p t s", t=5)), w=[b_const])
        DMA("pool", lambda h: h.dma_start(out=cmask[:], in_=cmd[:, :]), w=[b_const])
        DMA("act", lambda h: h.dma_start(out=dms[:], in_=dmsd.rearrange("p (g s) -> p g s", g=4)), w=[b_const])
        OP("dve", lambda h: h.memset(ones_bf[:], 1.0), w=[b_const])
        OP("dve", lambda h: h.tensor_copy(out=identb[:], in_=idf[:]), r=[b_const], w=[b_const])
        OP("dve", lambda h: h.tensor_scalar(out=nidentb[:], in0=idf[:], scalar1=-1.0, scalar2=None, op0=ALU.mult), r=[b_const], w=[b_const])

        wlist = []
        for l in range(DEPTH):
            for i in range(NWIN):
                wlist.append(("in", l * NWIN + i))
            for dch in range(16):
                wlist.append(("out", l * 16 + dch))
        wstate = {"issued": 0, "used": 0}

        def issue_w(upto):
            while wstate["issued"] < min(upto, len(wlist)):
                n = wstate["issued"]
                kind, idx = wlist[n]
                src = (wind if kind == "in" else woutd)[idx].rearrange("p (k c) -> p k c", k=16)
                slot = n % 3
                DMA("pool", lambda h, slot=slot, src=src: h.dma_start(out=wbuf[slot][:], in_=src), w=[b_w[slot]])
                wstate["issued"] += 1

        def next_w(ahead=2):
            n = wstate["used"]
            wstate["used"] += 1
            issue_w(n + 1 + ahead)
            return n % 3

        OB = (3, 7)
        bank_state = {"n": 0}

        def next_bank(nring=3):
            b = bank_state["n"] % nring
            bank_state["n"] += 1
            return b

        issue_w(2)

        def emit_V_block(blk, l_, vslots):
            bank = next_bank()
            for half in range(2):
                for kc in range(16):
                    OP("pe", lambda h, half=half, kc=kc, bank=bank, vs=vslots[half]: h.matmul(
                        ps[bank][:, half * 128:(half + 1) * 128], lhsT=hi[:, kc, blk * 128:(blk + 1) * 128],
                        rhs=wbuf[vs][:, kc, :], start=(kc == 0), stop=(kc == 15)),
                       r=[b_w[vslots[half]], b_hi[kc]], w=[b_ps[bank]], sig=(kc == 15 and half == 1))
            OP("dve", lambda h: h.tensor_copy(out=vv[:, blk, :], in_=ps[bank][:, 0:256]),
               r=[b_ps[bank]], w=[b_v[blk]])
            if blk == NPB - 1:
                OP("act", lambda h: h.copy(out=zr[2][:, 256:512], in_=ps[bank][:, 0:256]),
                   r=[b_ps[bank], b_v[blk]], w=[b_zr[2]])
                DMA("sp", lambda h: h.dma_start(out=vwpd[l_], in_=zr[2][:, 256:512]), r=[b_zr[2]], w=[b_out])

        vslots0 = [next_w(2), next_w(1)]

        def issue_x(blk):
            st = xst[blk % 5]
            bx = b_xst[blk % 5]
            if blk < NPB:
                DMA("sp", lambda h, st=st, blk=blk: h.dma_start(out=st[:, :], in_=xin[blk * 128:(blk + 1) * 128, :]), w=[bx])
            else:
                DMA("sp", lambda h, st=st: h.dma_start(out=st[0:NS, :], in_=xsd[:, :]), w=[bx])

        for blk in range(5):
            issue_x(blk)
        for blk in range(NPB + 1):
            st = xst[blk % 5]
            bx = b_xst[blk % 5]
            nrow = 128 if blk < NPB else NS
            c_lo = blk * 128
            for g in range(4):
                bank = next_bank()
                for j in range(4):
                    kc = g * 4 + j
                    OP("pe", lambda h, st=st, kc=kc, j=j, bank=bank, nrow=nrow: h.transpose(
                        out=ps[bank][:, j * 128:j * 128 + nrow], in_=st[0:nrow, kc * 128:(kc + 1) * 128],
                        identity=idf[0:nrow, 0:nrow]),
                       r=[bx, b_idf], w=[b_ps[bank]], sig=(j == 3))
                src = ps[bank][:, :].rearrange("p (j t) -> p j t", j=4)[:, :, 0:nrow]
                OP("act", lambda h, g=g, src=src, c_lo=c_lo, nrow=nrow: h.copy(
                    out=hi[:, g * 4:(g + 1) * 4, c_lo:c_lo + nrow], in_=src),
                   r=[b_ps[bank]], w=b_hi[g * 4:(g + 1) * 4])
                if blk > 0:
                    OP("dve", lambda h, g=g, src=src, c_lo=c_lo, nrow=nrow: h.tensor_tensor(
                        out=lo[:, g * 4:(g + 1) * 4, c_lo - 128:c_lo - 128 + nrow], in0=src,
                        in1=hi[:, g * 4:(g + 1) * 4, c_lo:c_lo + nrow], op=ALU.subtract),
                       r=[b_ps[bank]] + b_hi[g * 4:(g + 1) * 4], w=b_lo[g * 4:(g + 1) * 4])
            if blk + 5 < NPB + 1:
                issue_x(blk + 5)
            if 1 <= blk <= NPB:
                emit_V_block(blk - 1, 0, vslots0)

        def proj_piece(slot, s, n, bank):
            for kc in range(16):
                OP("pe", lambda h, slot=slot, kc=kc, s=s, n=n, bank=bank: h.matmul(
                    ps[bank][:, 0:n], lhsT=wbuf[slot][:, kc, :], rhs=hi[:, kc, s:s + n],
                    start=(kc == 0), stop=(kc == 15)),
                   r=[b_w[slot], b_hi[kc]], w=[b_ps[bank]], sig=(kc == 15))

        ckpt(1)
        for l in range(DEPTH):
            c0 = 128 * (l + 1)
            k0 = 128 * l
            full = pieces(c0, NT)
            last = (l == DEPTH - 1)

            for hf in range(2):
                DMA("sp", lambda h, l=l, hf=hf: h.dma_start(out=zr[hf][0:8, :], in_=scd[l][:, hf * 512:(hf + 1) * 512]), w=[b_zr[hf]])
            bank = next_bank()
            for j in range(8):
                OP("pe", lambda h, j=j, bank=bank: h.transpose(
                    out=ps[bank][:, j * 8:(j + 1) * 8], in_=zr[j // 4][0:8, (j % 4) * 128:(j % 4 + 1) * 128], identity=idf[0:8, 0:8]),
                   r=[b_zr[j // 4], b_const], w=[b_ps[bank]], sig=(j == 7))
            OP("act", lambda h, bank=bank: h.copy(
                out=usx[:, :, :, 0:2], in_=ps[bank][:, 0:64].rearrange("p (j b t) -> p j b t", j=8, b=4)),
               r=[b_ps[bank]], w=[b_usx])

            if l == 0:
                vslots = vslots0
            else:
                vslots = [next_w(2), next_w(1)]
                for blk in range(l, NPB):
                    emit_V_block(blk, l, vslots)
            for bb in range(4):
                bank = next_bank()
                for half in range(2):
                    for kc in range(16):
                        OP("pe", lambda h, half=half, kc=kc, bank=bank, bb=bb, vs=vslots[half]: h.matmul(
                            ps[bank][0:4, half * 128:(half + 1) * 128], lhsT=hi[:, kc, NP + bb * 4:NP + bb * 4 + 4],
                            rhs=wbuf[vs][:, kc, :], start=(kc == 0), stop=(kc == 15)),
                           r=[b_w[vslots[half]], b_hi[kc]], w=[b_ps[bank]], sig=(kc == 15 and half == 1))
                OP("dve", lambda h, bank=bank, bb=bb: h.tensor_copy(out=vnew_bf[bb], in_=ps[bank][0:4, 0:256]),
                   r=[b_ps[bank]], w=[b_vnew[bb]])
                OP("act", lambda h, bank=bank, bb=bb: h.copy(out=zr[bb % 2][0:4, 0:256], in_=ps[bank][0:4, 0:256]),
                   r=[b_ps[bank], b_vnew[bb]], w=[b_zr[bb % 2]])
                DMA("sp", lambda h, l=l, bb=bb: h.dma_start(out=vwsd[l, bb, 124:128, :], in_=zr[bb % 2][0:4, 0:256]),
                    r=[b_zr[bb % 2]], w=[b_out])

            for kc2 in range(2):
                slot = next_w()
                for (s, n) in [(k0, 128)] + full:
                    bank = next_bank()
                    proj_piece(slot, s, n, bank)
                    OP("act", lambda h, kc2=kc2, s=s, n=n, bank=bank: h.copy(out=kT[:, kc2, s:s + n], in_=ps[bank][:, 0:n]),
                       r=[b_ps[bank]], w=[b_kT[kc2]])
                bank = next_bank()
                for kc in range(16):
                    OP("pe", lambda h, slot=slot, kc=kc, bank=bank: h.matmul(
                        ps[bank][:, 0:128], lhsT=hi[:, kc, NP - 128:NP], rhs=wbuf[slot][:, kc, :],
                        start=(kc == 0), stop=(kc == 15)),
                       r=[b_w[slot], b_hi[kc]], w=[b_ps[bank]], sig=(kc == 15))
                OP("act", lambda h, kc2=kc2, bank=bank: h.copy(out=zr[2][:, kc2 * 128:(kc2 + 1) * 128], in_=ps[bank][:, 0:128]),
                   r=[b_ps[bank]], w=[b_zr[2]])
                bank = next_bank()
                for bb in range(4):
                    for kc in range(16):
                        OP("pe", lambda h, slot=slot, kc=kc, bank=bank, bb=bb: h.matmul(
                            ps[bank][0:4, bb * 128:(bb + 1) * 128], lhsT=hi[:, kc, NP + bb * 4:NP + bb * 4 + 4],
                            rhs=wbuf[slot][:, kc, :], start=(kc == 0), stop=(kc == 15)),
                           r=[b_w[slot], b_hi[kc]], w=[b_ps[bank]], sig=(kc == 15 and bb == 3))
                OP("dve", lambda h, kc2=kc2, bank=bank: h.tensor_copy(out=zr[kc2][0:4, :], in_=ps[bank][0:4, :]),
                   r=[b_ps[bank]], w=[b_zr[kc2]])
                DMA("sp", lambda h, l=l, kc2=kc2: h.dma_start(
                    out=kwsd[l, :, 124:128, kc2 * 128:(kc2 + 1) * 128].rearrange("b t f -> t b f"),
                    in_=zr[kc2][0:4, :].rearrange("p (b c) -> p b c", b=4)), r=[b_zr[kc2]], w=[b_out])
            DMA("sp", lambda h, l=l: h.dma_start(out=kwpd[l], in_=zr[2][:, 0:256]), r=[b_zr[2]], w=[b_out])
            DMA("sp", lambda h, l=l: h.dma_start(out=kwsd[l, :, 0:124, :], in_=ckd[l, :, 4:128, :]), w=[b_out])
            DMA("sp", lambda h, l=l: h.dma_start(out=vwsd[l, :, 0:124, :], in_=cvd[l, :, 4:128, :]), w=[b_out])

            ckpt(2 + 10 * l)
            ckpt(3 + 10 * l)
            zcount = [0]

            def gen_G(c):
                slot = next_w()
                for (s, n) in full:
                    bank = next_bank()
                    proj_piece(slot, s, n, bank)
                    zi = zcount[0] % 2
                    zcount[0] += 1
                    OP("act", lambda h, n=n, bank=bank, zi=zi: h.activation(
                        out=zr[zi][:, 0:n], in_=ps[bank][:, 0:n], func=AF.Tanh, scale=0.5),
                       r=[b_ps[bank]], w=[b_zr[zi]])
                    OP("dve", lambda h, c=c, s=s, n=n, bank=bank, zi=zi: h.scalar_tensor_tensor(
                        out=mixT[:, c, s - 128:s - 128 + n], in0=zr[zi][:, 0:n], scalar=1.0, in1=ps[bank][:, 0:n],
                        op0=ALU.add, op1=ALU.mult),
                       r=[b_zr[zi], b_ps[bank]], w=[b_mix[c]])
                    yield

            def gen_Q(c):
                slot = next_w()
                qb = c % 2
                for (s, n) in full:
                    bank = next_bank()
                    proj_piece(slot, s, n, bank)
                    OP("act", lambda h, qb=qb, s=s, n=n, bank=bank: h.activation(
                        out=qbuf[qb][:, s - 128:s - 128 + n], in_=ps[bank][:, 0:n], func=AF.Copy, scale=0.125),
                       r=[b_ps[bank]], w=[b_q[qb]])
                    yield
                OP("act", lambda h, qb=qb, c=c: h.copy(
                    out=qs[:, :, c // 4, (c % 4) * 4:(c % 4) * 4 + 4],
                    in_=qbuf[qb][:, MC - NS:MC].rearrange("p (b t) -> p b t", b=4)),
                   r=[b_q[qb]], w=[b_qs])

            def gen_convA(j):
                cbase = l * 24 + j
                slot = next_w()
                for (s, n) in pieces(c0 - 2, NT):
                    bank = next_bank()
                    proj_piece(slot, s, n, bank)
                    OP("act", lambda h, s=s, n=n, bank=bank: h.copy(out=scrA[:, s - 126:s - 126 + n], in_=ps[bank][:, 0:n]),
                       r=[b_ps[bank]], w=[b_scrA])
                    yield
                slot = next_w()
                for (s, n) in pieces(c0 - 2, NT):
                    bank = next_bank()
                    proj_piece(slot, s, n, bank)
                    OP("dve", lambda h, s=s, n=n, bank=bank: h.tensor_tensor(
                        out=scrB[:, s - 126:s - 126 + n], in0=scrA[:, s - 126:s - 126 + n], in1=ps[bank][:, 0:n], op=ALU.mult),
                       r=[b_ps[bank], b_scrA], w=[b_scrB])
                    yield
                OP("dve", lambda h, c0=c0: h.tensor_tensor(
                    out=scrB[:, c0 - 128:512 - 126], in0=scrB[:, c0 - 128:512 - 126], in1=cmask[:, c0 - 128:386], op=ALU.mult),
                   r=[b_const], w=[b_scrB])
                npc = NP - c0
                a0 = c0 - 126
                OP("act", lambda h: h.activation(out=scrA[:, a0:a0 + npc], in_=scrB[:, a0 - 2:a0 - 2 + npc],
                                                 func=AF.Identity, scale=cw[:, cbase:cbase + 1]),
                   r=[b_scrB, b_const], w=[b_scrA])
                OP("dve", lambda h: h.scalar_tensor_tensor(
                    out=scrA[:, a0:a0 + npc], in0=scrB[:, a0 - 1:a0 - 1 + npc], scalar=cw[:, cbase + 8:cbase + 9],
                    in1=scrA[:, a0:a0 + npc], op0=ALU.mult, op1=ALU.add),
                   r=[b_scrB, b_const], w=[b_scrA])
                OP("dve", lambda h: h.scalar_tensor_tensor(
                    out=scrA[:, a0:a0 + npc], in0=scrB[:, a0:a0 + npc], scalar=cw[:, cbase + 16:cbase + 17],
                    in1=scrA[:, a0:a0 + npc], op0=ALU.mult, op1=ALU.add),
                   r=[b_scrB, b_const], w=[b_scrA])
                us = NP - 126
                OP("dve", lambda h: h.tensor_copy(out=usx[:, j, :, 2:6], in_=scrB[:, us:us + NS].rearrange("p (b t) -> p b t", b=4)),
                   r=[b_scrB], w=[b_usx])
                OP("dve", lambda h: h.tensor_scalar(out=ysx[:, :, :], in0=usx[:, j, :, 0:4], scalar1=cw[:, cbase:cbase + 1],
                                                     scalar2=None, op0=ALU.mult),
                   r=[b_usx, b_const], w=[b_ysx])
                OP("dve", lambda h: h.scalar_tensor_tensor(
                    out=ysx[:, :, :], in0=usx[:, j, :, 1:5], scalar=cw[:, cbase + 8:cbase + 9], in1=ysx[:, :, :],
                    op0=ALU.mult, op1=ALU.add), r=[b_usx, b_const], w=[b_ysx])
                OP("dve", lambda h: h.scalar_tensor_tensor(
                    out=scrA[:, us:us + NS].rearrange("p (b t) -> p b t", b=4), in0=usx[:, j, :, 2:6],
                    scalar=cw[:, cbase + 16:cbase + 17], in1=ysx[:, :, :], op0=ALU.mult, op1=ALU.add),
                   r=[b_usx, b_ysx, b_const], w=[b_scrA])
                OP("dve", lambda h: h.tensor_copy(out=ucol[:, j, 0:2], in_=scrB[:, us - 2:us]), r=[b_scrB], w=[b_ucol])
                OP("dve", lambda h: h.tensor_copy(
                    out=ucol[:, j, 2:10].rearrange("p (b t) -> p b t", b=4),
                    in_=scrB[:, us:us + NS].rearrange("p (b t) -> p b t", b=4)[:, :, 2:4]), r=[b_scrB], w=[b_ucol])

            def gen_convB(j):
                slot = next_w()
                for (s, n) in full:
                    bank = next_bank()
                    proj_piece(slot, s, n, bank)
                    OP("dve", lambda h, s=s, n=n, bank=bank: h.tensor_tensor(
                        out=scrA[:, s - 126:s - 126 + n], in0=ps[bank][:, 0:n], in1=scrA[:, s - 126:s - 126 + n],
                        op=ALU.mult),
                       r=[b_ps[bank]], w=[b_scrA])
                    yield
                slot = next_w()
                for pi, (s, n) in enumerate(full):
                    bank = next_bank()
                    proj_piece(slot, s, n, bank)
                    zi = zcount[0] % 2
                    zcount[0] += 1
                    OP("act", lambda h, n=n, bank=bank, zi=zi: h.activation(
                        out=zr[zi][:, 0:n], in_=ps[bank][:, 0:n], func=AF.Tanh, scale=0.5),
                       r=[b_ps[bank]], w=[b_zr[zi]])
                    OP("dve", lambda h, n=n, bank=bank, zi=zi: h.scalar_tensor_tensor(
                        out=zr[zi][:, 0:n], in0=zr[zi][:, 0:n], scalar=1.0, in1=ps[bank][:, 0:n], op0=ALU.add, op1=ALU.mult),
                       r=[b_ps[bank]], w=[b_zr[zi]])
                    OP("dve", lambda h, s=s, n=n, zi=zi, j=j: h.scalar_tensor_tensor(
                        out=mixT[:, 8 + j, s - 128:s - 128 + n], in0=zr[zi][:, 0:n], scalar=0.5,
                        in1=scrA[:, s - 126:s - 126 + n], op0=ALU.mult, op1=ALU.mult),
                       r=[b_scrA, b_zr[zi]], w=[b_mix[8 + j]])
                    yield

            def chain(*gens):
                for g in gens:
                    for _ in g:
                        yield

            steps = [(c, bi) for c in range(8) for bi in range(l + 1, NPB)]
            nsteps = len(steps)
            per_c = NPB - 1 - l

            def st_S(i):
                c, bi = steps[i]
                kc2 = c // 4
                r = i % 3
                for e in range(2):
                    OP("pe", lambda h, e=e, c=c, bi=bi, kc2=kc2: h.matmul(
                        ps[4 + e][:, 0:256], lhsT=qbuf[c % 2][e * 64:(e + 1) * 64, (bi - 1) * 128:bi * 128],
                        rhs=kT[e * 64:(e + 1) * 64, kc2, (bi - 1) * 128:(bi + 1) * 128], start=True, stop=True),
                       r=[b_q[c % 2], b_kT[kc2]], w=[b_ps[4 + e]], sig=True)
                if i == 0: ckpt(3.21 + 10 * l)
                sk = l * 16 + c * 2
                OP("act", lambda h, r=r, sk=sk: h.copy(out=Sp[r][:, :, 256:257], in_=sinkbc[:, sk:sk + 2].rearrange("p (e o) -> p e o", o=1)),
                   r=[b_const], w=[b_Sp[r]])
                tbl = bi if bi <= 4 else 0
                for e in range(2):
                    sl = slope(head_of(c, e))
                    OP("dve", lambda h, e=e, r=r, tbl=tbl, sl=sl: h.scalar_tensor_tensor(
                        out=Sp[r][:, e, 0:256], in0=dm[:, tbl, :], scalar=sl, in1=ps[4 + e][:, 0:256],
                        op0=ALU.mult, op1=ALU.add),
                       r=[b_ps[4 + e], b_const], w=[b_Sp[r]])
                if i == 0: ckpt(3.22 + 10 * l)
                if i == 0: ckpt(3.23 + 10 * l)
                OP("dve", lambda h, r=r: h.tensor_reduce(out=st_negm[r][:, :], in_=Sp[r][:, :, :], axis=AX.X, op=ALU.max, negate=True),
                   r=[b_Sp[r]], w=[b_st[r]])
                if i == 0: ckpt(3.24 + 10 * l)
                for e in range(2):
                    OP("act", lambda h, e=e, r=r: h.activation(
                        out=Sp[r][:, e, :], in_=Sp[r][:, e, :], func=AF.Exp, bias=st_negm[r][:, e:e + 1], scale=1.0,
                        accum_out=st_den[r][:, e:e + 1]),
                       r=[b_Sp[r], b_st[r]], w=[b_Sp[r], b_st[r]])
                if i == 0: ckpt(3.25 + 10 * l)
                OP("dve", lambda h, r=r: h.reciprocal(out=st_rden[r][:, :], in_=st_den[r][:, :]), r=[b_st[r]], w=[b_st[r]])
                for e in range(2):
                    OP("dve", lambda h, e=e, r=r: h.tensor_scalar(
                        out=Spb[r][:, e, 0:256], in0=Sp[r][:, e, 0:256], scalar1=st_rden[r][:, e:e + 1], scalar2=None,
                        op0=ALU.mult),
                       r=[b_Sp[r], b_st[r]], w=[b_Sp[r]])

            def st_T(i):
                r = i % 3
                rp = i % 2
                for e in range(2):
                    for sc in range(2):
                        OP("pe", lambda h, e=e, sc=sc, r=r: h.transpose(
                            out=ps6b[:, (e * 2 + sc) * 128:(e * 2 + sc + 1) * 128], in_=Spb[r][:, e, sc * 128:(sc + 1) * 128],
                            identity=identb[:, :]),
                           r=[b_Sp[r], b_const], w=[b_ps[6]], sig=(e == 1 and sc == 1))
                OP("act", lambda h, rp=rp: h.copy(out=PTsb[rp][:, :, :, :], in_=ps6b[:, 0:512].rearrange("p (e s q) -> p e s q", e=2, s=2)),
                   r=[b_ps[6]], w=[b_PTsb[rp]])

            def st_PV(i):
                c, bi = steps[i]
                kc2 = c // 4
                r = i % 2
                for e in range(2):
                    for sc in range(2):
                        OP("pe", lambda h, e=e, sc=sc, r=r, bi=bi, kc2=kc2: h.matmul(
                            ps[OB[r]][:, e * 128:(e + 1) * 128],
                            lhsT=vv[:, bi - 1 + sc, kc2 * 128:(kc2 + 1) * 128], rhs=PTsb[r][:, e, sc, :],
                            start=(sc == 0), stop=(sc == 1)),
                           r=[b_v[bi - 1 + sc], b_PTsb[r]], w=[b_ps[OB[r]]], sig=(e == 1 and sc == 1))
                for e in range(2):
                    OP("dve", lambda h, e=e, r=r, c=c, bi=bi: h.scalar_tensor_tensor(
                        out=mixT[e * 64:(e + 1) * 64, c, (bi - 1) * 128:bi * 128],
                        in0=ps[OB[r]][e * 64:(e + 1) * 64, e * 128:(e + 1) * 128], scalar=0.5,
                        in1=mixT[e * 64:(e + 1) * 64, c, (bi - 1) * 128:bi * 128], op0=ALU.mult, op1=ALU.mult),
                       r=[b_ps[OB[r]]], w=[b_mix[c]])

            def sL(bb, l=l):
                rr = bb % 2
                kst_r = zr[2][:, 0:256]
                vst_r = zr[2][:, 256:512]
                vsb = qbuf[0][:, rr * 256:(rr + 1) * 256]
                DMA("sp", lambda h: h.dma_start(out=kst_r, in_=ckd[l, bb]), w=[b_zr[2]])
                DMA("sp", lambda h: h.dma_start(out=vst_r, in_=cvd[l, bb]), w=[b_zr[2]])
                OP("dve", lambda h: h.tensor_copy(out=vsb, in_=vst_r), r=[b_zr[2]], w=[b_vsbf2[rr], b_q[0]])

            def sA(bb, l=l):
                rr = bb % 2
                kst_r = zr[2][:, 0:256]
                vst_r = zr[2][:, 256:512]
                vsb = qbuf[0][:, rr * 256:(rr + 1) * 256]
                ksT_r = qbuf[0][:, 512 + rr * 264:512 + (rr + 1) * 264].rearrange("p (k s) -> p k s", k=2)
                Ss_r = Ss1
                bS = b_Ss1
                ng = s_negm2[:, rr * 4:(rr + 1) * 4]
                dn = s_den2[:, rr * 4:(rr + 1) * 4]
                rd = s_rden2[:, rr * 4:(rr + 1) * 4]
                for kc2 in range(2):
                    OP("pe", lambda h, kc2=kc2: h.transpose(out=ps[6][:, kc2 * 128:(kc2 + 1) * 128],
                                                            in_=kst_r[:, kc2 * 128:(kc2 + 1) * 128], identity=idf[:, :]),
                       r=[b_zr[2], b_const], w=[b_ps[6]], sig=(kc2 == 1))
                OP("act", lambda h: h.copy(out=ksT_r[:, :, 0:128], in_=ps[6][:, 0:256].rearrange("p (k s) -> p k s", k=2)),
                   r=[b_ps[6]], w=[b_ksT2[rr], b_q[0]])
                OP("dve", lambda h: h.tensor_copy(out=ksT_r[:, :, 128:132], in_=kT[:, :, NP + bb * 4:NP + bb * 4 + 4]),
                   r=b_kT, w=[b_ksT2[rr], b_q[0]])
                for g in range(4):
                    kc2, e = divmod(g, 2)
                    bk = 4 + e
                    OP("pe", lambda h, kc2=kc2, e=e, bk=bk: h.matmul(
                        ps[bk][0:16, kc2 * 132:(kc2 + 1) * 132],
                        lhsT=qs[e * 64:(e + 1) * 64, bb, kc2, :],
                        rhs=ksT_r[e * 64:(e + 1) * 64, kc2, :], start=True, stop=True),
                       r=[b_qs, b_ksT2[rr], b_q[0]], w=[b_ps[bk]], sig=True)
                for g in range(4):
                    kc2, e = divmod(g, 2)
                    bk = 4 + e
                    OP("dve", lambda h, g=g, kc2=kc2, bk=bk: h.tensor_tensor(
                        out=Ss_r[:, g, 0:132], in0=ps[bk][0:16, kc2 * 132:(kc2 + 1) * 132],
                        in1=dms[:, g, :], op=ALU.add),
                       r=[b_ps[bk], b_const], w=[bS])
                OP("dve", lambda h: h.tensor_copy(out=Ss_r[:, :, 132:133], in_=sinkrow[:, l * 4:(l + 1) * 4].rearrange("p (g o) -> p g o", o=1)),
                   r=[b_const], w=[bS])
                OP("dve", lambda h: h.tensor_reduce(out=ng, in_=Ss_r[:, :, :], axis=AX.X, op=ALU.max, negate=True),
                   r=[bS], w=[b_sst2[rr]])
                for g in range(4):
                    OP("act", lambda h, g=g: h.activation(out=Ss_r[:, g, :], in_=Ss_r[:, g, :], func=AF.Exp,
                                                          bias=ng[:, g:g + 1], scale=1.0, accum_out=dn[:, g:g + 1]),
                       r=[bS, b_sst2[rr]], w=[bS, b_sst2[rr]])
                OP("dve", lambda h: h.reciprocal(out=rd, in_=dn), r=[b_sst2[rr]], w=[b_sst2[rr]])
                for g in range(4):
                    OP("dve", lambda h, g=g: h.tensor_scalar(out=Ss_r[:, g, 0:132], in0=Ss_r[:, g, 0:132],
                                                             scalar1=rd[:, g:g + 1], scalar2=None, op0=ALU.mult),
                       r=[bS, b_sst2[rr]], w=[bS])

            def sB(bb, l=l):
                rr = bb % 2
                Ss_r = Ss1
                bS = b_Ss1
                for g in range(4):
                    OP("pe", lambda h, g=g: h.transpose(out=ps[6][:, 256 + g * 16:256 + (g + 1) * 16], in_=Ss_r[:, g, 0:128],
                                                        identity=idf[0:16, 0:16]),
                       r=[bS, b_const], w=[b_ps[6]], sig=False)
                    OP("pe", lambda h, g=g: h.transpose(out=ps[6][0:4, 320 + g * 16:320 + (g + 1) * 16], in_=Ss_r[:, g, 128:132],
                                                        identity=idf[0:16, 0:16]),
                       r=[bS, b_const], w=[b_ps[6]], sig=(g == 3))
                OP("act", lambda h: h.copy(out=PTs2[:, rr, :, :], in_=ps[6][:, 256:320].rearrange("p (g q) -> p g q", g=4)),
                   r=[b_ps[6]], w=[b_PTs2[rr]])
                OP("act", lambda h: h.copy(out=PT22[:, rr, :, :], in_=ps[6][0:4, 320:384].rearrange("p (g q) -> p g q", g=4)),
                   r=[b_ps[6]], w=[b_PTs2[rr]])

            def sC(bb, l=l):
                rr = bb % 2
                vsb = qbuf[0][:, rr * 256:(rr + 1) * 256]
                ob = OB[rr]
                for g in range(4):
                    kc2, e = divmod(g, 2)
                    OP("pe", lambda h, g=g, kc2=kc2: h.matmul(
                        ps[ob][:, g * 16:(g + 1) * 16], lhsT=vsb[:, kc2 * 128:(kc2 + 1) * 128], rhs=PTs2[:, rr, g, :],
                        start=True, stop=False), r=[b_vsbf2[rr], b_q[0], b_PTs2[rr]], w=[b_ps[ob]], sig=False)
                    OP("pe", lambda h, g=g, kc2=kc2: h.matmul(
                        ps[ob][:, g * 16:(g + 1) * 16], lhsT=vnew_bf_t[0:4, bb, kc2 * 128:(kc2 + 1) * 128], rhs=PT22[0:4, rr, g, :],
                        start=False, stop=True), r=[b_vnew[bb], b_PTs2[rr]], w=[b_ps[ob]], sig=(g == 3))
                for g in range(4):
                    kc2, e = divmod(g, 2)
                    mc0 = MC - NS + bb * 4
                    OP("dve", lambda h, g=g, kc2=kc2, e=e, mc0=mc0: h.scalar_tensor_tensor(
                        out=mixT[e * 64:(e + 1) * 64, kc2 * 4:(kc2 + 1) * 4, mc0:mc0 + 4],
                        in0=ps[ob][e * 64:(e + 1) * 64, g * 16:(g + 1) * 16].rearrange("p (c t) -> p c t", c=4), scalar=0.5,
                        in1=mixT[e * 64:(e + 1) * 64, kc2 * 4:(kc2 + 1) * 4, mc0:mc0 + 4], op0=ALU.mult, op1=ALU.mult),
                       r=[b_ps[ob]], w=b_mix[kc2 * 4:(kc2 + 1) * 4])

            samp_sched = [[(sL, 0)], [(sA, 0), (sL, 1)], [(sB, 0)], [(sC, 0), (sA, 1), (sL, 2)], [(sB, 1)], [(sC, 1), (sA, 2), (sL, 3)],
                          [(sB, 2)], [(sC, 2), (sA, 3)], [(sB, 3)], [(sC, 3)]]

            for _ in chain(gen_G(0), gen_Q(0)):
                pass
            ckpt(3.1 + 10 * l)
            filler = None
            fill_left = 0
            for i in range(nsteps + 3):
                if i < nsteps:
                    c, bi = steps[i]
                    if bi == l + 1:
                        if filler is not None:
                            for _ in filler:
                                pass
                        gens = [gen_convA(c)]
                        if c + 1 < 8:
                            gens += [gen_G(c + 1), gen_Q(c + 1)]
                        gens.append(gen_convB(c))
                        filler = chain(*gens)
                        fill_left = 3 * (2 if c + 1 < 8 else 0) + 12
                        steps_left = per_c + (3 if c == 7 else 0)
                        fill_tot = fill_left
                        steps_tot = steps_left
                        step_k = 0
                    if c == 1 and bi == l + 1:
                        ckpt(3.6 + 10 * l)
                    st_S(i)
                if steps_left > 0:
                    npull = ((step_k + 1) * fill_tot + steps_tot - 1) // steps_tot - (step_k * fill_tot + steps_tot - 1) // steps_tot
                    step_k += 1
                    n_first = (npull + 1) // 2
                    n_second = npull - n_first
                    for _ in range(n_first):
                        try:
                            next(filler)
                        except StopIteration:
                            break
                    fill_left -= npull
                    steps_left -= 1
                else:
                    n_second = 0
                if 0 <= i - 2 < nsteps:
                    st_T(i - 2)
                if 0 <= i - 3 < nsteps:
                    st_PV(i - 3)
                for _ in range(n_second):
                    try:
                        next(filler)
                    except StopIteration:
                        break
                si_ = i - (nsteps + 3 - len(samp_sched))
                if 0 <= si_ < len(samp_sched):
                    for fn_, bb_ in samp_sched[si_]:
                        fn_(bb_)
            for _ in filler:
                pass

            ckpt(4 + 10 * l)
            for half in range(2):
                bank = next_bank()
                for jj in range(4):
                    j = half * 4 + jj
                    OP("pe", lambda h, j=j, jj=jj, bank=bank: h.transpose(
                        out=ps[bank][0:10, jj * 128:(jj + 1) * 128], in_=ucol[:, j, :], identity=idf[:, :]),
                       r=[b_ucol, b_const], w=[b_ps[bank]], sig=(jj == 3))
                OP("act", lambda h, half=half, bank=bank: h.copy(out=zr[half][0:10, :], in_=ps[bank][0:10, :]),
                   r=[b_ps[bank]], w=[b_zr[half]])
                DMA("sp", lambda h, l=l, half=half: h.dma_start(out=cvpd[l][:, half * 512:(half + 1) * 512], in_=zr[half][0:2, :]),
                    r=[b_zr[half]], w=[b_out])
                DMA("sp", lambda h, l=l, half=half: h.dma_start(out=cvsd[l][:, half * 512:(half + 1) * 512], in_=zr[half][2:10, :]),
                    r=[b_zr[half]], w=[b_out])

            ckpt(5 + 10 * l)

            ckpt(6 + 10 * l)
            pend = []

            def emit_stats(item):
                dch, pi, s, n, zi, qi = item
                OP("pe", lambda h: h.matmul(ps[2 + pi][:, 0:n], lhsT=ones_bf[:, :], rhs=hi[:, dch, s:s + n],
                                            start=(dch == 0), stop=False),
                   r=[b_hip[dch][pi], b_const], w=[b_ps[2 + pi]], sig=False)
                OP("pe", lambda h: h.matmul(ps[2 + pi][:, 0:n], lhsT=ones_bf[:, :], rhs=lo[:, dch, s - 128:s - 128 + n],
                                            start=False, stop=(dch == 15)),
                   r=[b_lop[dch][pi], b_const], w=[b_ps[2 + pi]], sig=(dch == 15))
                OP("pe", lambda h: h.matmul(ps[5 + pi][:, 0:n], lhsT=ones_bf[:, :], rhs=sqb[qi][:, 0:n],
                                            start=(dch == 0), stop=(dch == 15)),
                   r=[b_sqb[qi], b_const], w=[b_ps[5 + pi]], sig=True)

            cnt = 0
            for dch in range(16):
                slot = next_w()
                for pi, (s, n) in enumerate(full):
                    bank = cnt % 2
                    zi = cnt % 3
                    qi = cnt % 3
                    cnt += 1
                    for kc in range(16):
                        OP("pe", lambda h, kc=kc, s=s, n=n, bank=bank, slot=slot: h.matmul(
                            ps[bank][:, 0:n], lhsT=wbuf[slot][:, kc, :], rhs=mixT[:, kc, s - 128:s - 128 + n],
                            start=(kc == 0), stop=(kc == 15)),
                           r=[b_w[slot], b_mix[kc]], w=[b_ps[bank]], sig=(kc == 15))
                    Z = zr[zi]
                    OP("dve", lambda h, Z=Z, dch=dch, s=s, n=n, bank=bank: h.scalar_tensor_tensor(
                        out=Z[:, 0:n], in0=hi[:, dch, s:s + n], scalar=ALPHA, in1=ps[bank][:, 0:n], op0=ALU.mult, op1=ALU.add),
                       r=[b_hi[dch], b_ps[bank]], w=[b_zr[zi]])
                    OP("dve", lambda h, Z=Z, dch=dch, s=s, n=n: h.scalar_tensor_tensor(
                        out=Z[:, 0:n], in0=lo[:, dch, s - 128:s - 128 + n], scalar=ALPHA, in1=Z[:, 0:n], op0=ALU.mult, op1=ALU.add),
                       r=[b_lo[dch]], w=[b_zr[zi]])
                    OP("act", lambda h, Z=Z, dch=dch, s=s, n=n: h.copy(out=hi[:, dch, s:s + n], in_=Z[:, 0:n]),
                       r=[b_zr[zi]], w=[b_hi[dch], b_hip[dch][pi]])
                    OP("act", lambda h, Z=Z, n=n, qi=qi: h.activation(out=sqb[qi][:, 0:n], in_=Z[:, 0:n], func=AF.Square),
                       r=[b_zr[zi]], w=[b_sqb[qi]])
                    OP("dve", lambda h, Z=Z, dch=dch, s=s, n=n: h.tensor_tensor(
                        out=lo[:, dch, s - 128:s - 128 + n], in0=Z[:, 0:n], in1=hi[:, dch, s:s + n], op=ALU.subtract),
                       r=[b_zr[zi], b_hip[dch][pi]], w=[b_lo[dch], b_lop[dch][pi]])
                    pend.append((dch, pi, s, n, zi, qi))
                    if len(pend) > 2:
                        emit_stats(pend.pop(0))
            while pend:
                emit_stats(pend.pop(0))

            ckpt(7 + 10 * l)
            PCS = [(pi, s, n, s - 128, zr[pi]) for pi, (s, n) in enumerate(full)]
            for (pi, s, n, a, T) in PCS:
                OP("dve", lambda h, n=n, pi=pi, T=T: h.tensor_scalar(out=T[:, 0:n], in0=ps[2 + pi][:, 0:n],
                                                                     scalar1=1.0 / D, scalar2=None, op0=ALU.mult),
                   r=[b_ps[2 + pi]], w=[b_zr[pi]])
            for (pi, s, n, a, T) in PCS:
                OP("act", lambda h, a=a, n=n, T=T: h.copy(out=MH[:, a:a + n], in_=T[:, 0:n]), r=[b_zr[pi]], w=[b_scrA])
            for (pi, s, n, a, T) in PCS:
                OP("dve", lambda h, a=a, n=n, T=T: h.tensor_tensor(out=ML[:, a:a + n], in0=T[:, 0:n], in1=MH[:, a:a + n], op=ALU.subtract),
                   r=[b_zr[pi]], w=[b_scrA])
                OP("dve", lambda h, n=n, T=T: h.tensor_tensor(out=T[:, 0:n], in0=T[:, 0:n], in1=T[:, 0:n], op=ALU.mult),
                   r=[], w=[b_zr[pi]])
                OP("dve", lambda h, n=n, T=T, pi=pi: h.scalar_tensor_tensor(
                    out=T[:, 0:n], in0=ps[5 + pi][:, 0:n], scalar=1.0 / D, in1=T[:, 0:n], op0=ALU.mult, op1=ALU.subtract),
                   r=[b_ps[5 + pi]], w=[b_zr[pi]])
                OP("dve", lambda h, n=n, T=T: h.tensor_scalar(out=T[:, 0:n], in0=T[:, 0:n], scalar1=EPS, scalar2=None, op0=ALU.add),
                   r=[], w=[b_zr[pi]])
            for (pi, s, n, a, T) in PCS:
                OP("act", lambda h, n=n, T=T: h.activation(out=T[:, 0:n], in_=T[:, 0:n], func=AF.Sqrt),
                   r=[], w=[b_zr[pi]])
            for (pi, s, n, a, T) in PCS:
                OP("dve", lambda h, a=a, n=n, T=T: h.reciprocal(out=scrB[:, a:a + n], in_=T[:, 0:n]),
                   r=[b_zr[pi]], w=[b_scrB])

            ckpt(8 + 10 * l)
            opend = []
            ocount = [0]

            def emit_out(item):
                T, zi, s, n, dch = item
                bank = 4 + ocount[0] % 2
                og = ocount[0] % 8
                ocount[0] += 1
                if s < NP:
                    nb = n // 128
                    for j in range(nb):
                        OP("pe", lambda h, j=j: h.transpose(
                            out=ps[bank][:, j * 128:(j + 1) * 128], in_=T[:, j * 128:(j + 1) * 128], identity=idf[:, :]),
                           r=[b_zr[zi], b_const], w=[b_ps[bank]], sig=(j == nb - 1))
                    OP("dve", lambda h: h.tensor_copy(
                        out=ostage8[og][:, 0:nb, :], in_=ps[bank][:, 0:nb * 128].rearrange("p (j f) -> p j f", j=nb)),
                       r=[b_ps[bank]], w=[b_ost8[og]])
                    r0 = s - 512
                    DMA("sp", lambda h: h.dma_start(
                        out=yd[r0:r0 + nb * 128, dch * 128:(dch + 1) * 128].rearrange("(j p) f -> p j f", p=128),
                        in_=ostage8[og][:, 0:nb, :]), r=[b_ost8[og]], w=[b_out])
                else:
                    OP("pe", lambda h: h.transpose(out=ps[bank][0:NS, 0:128], in_=T[:, 0:NS], identity=idf[:, :]),
                       r=[b_zr[zi], b_const], w=[b_ps[bank]])
                    OP("dve", lambda h: h.tensor_copy(out=ostage8[og][0:NS, 0, :], in_=ps[bank][0:NS, 0:128]),
                       r=[b_ps[bank]], w=[b_ost8[og]])
                    DMA("sp", lambda h: h.dma_start(
                        out=ysd[:, dch * 128:(dch + 1) * 128], in_=ostage8[og][0:NS, 0, :]), r=[b_ost8[og]], w=[b_out])

            cnt = 0
            ocnt = 0
            for dch in range(16):
                gcol = l * 16 + dch
                for pi, (s, n) in enumerate(full):
                    a = s - 128
                    zi = cnt % 3
                    cnt += 1
                    T = zr[zi]
                    nb_ = cnt % 4
                    for mi, (lh, rh, bufs) in enumerate((
                            (identb, hi[:, dch, s:s + n], [b_hip[dch][pi]]),
                            (identb, lo[:, dch, s - 128:s - 128 + n], [b_lop[dch][pi]]),
                            (nidentb, MH[:, a:a + n], [b_scrA]),
                            (nidentb, ML[:, a:a + n], [b_scrA]))):
                        OP("pe", lambda h, lh=lh, rh=rh, n=n, nb_=nb_, mi=mi: h.matmul(
                            ps[nb_][:, 0:n], lhsT=lh[:, :], rhs=rh, start=(mi == 0), stop=(mi == 3)),
                           r=bufs + [b_const], w=[b_ps[nb_]], sig=(mi == 3))
                    OP("dve", lambda h, T=T, a=a, n=n, nb_=nb_: h.tensor_tensor(
                        out=T[:, 0:n], in0=ps[nb_][:, 0:n], in1=scrB[:, a:a + n], op=ALU.mult),
                       r=[b_ps[nb_], b_scrB], w=[b_zr[zi]])
                    OP("act", lambda h, T=T, n=n, gcol=gcol: h.activation(
                        out=T[:, 0:n], in_=T[:, 0:n], func=AF.Identity, bias=lnb[:, gcol:gcol + 1], scale=lng[:, gcol:gcol + 1]),
                       r=[b_const], w=[b_zr[zi]])
                    if not last:
                        OP("dve", lambda h, T=T, dch=dch, s=s, n=n: h.tensor_copy(out=hi[:, dch, s:s + n], in_=T[:, 0:n]),
                           r=[b_zr[zi]], w=[b_hi[dch], b_hip[dch][pi]])
                        OP("pool", lambda h, T=T, dch=dch, s=s, n=n: h.tensor_tensor(
                            out=lo[:, dch, s - 128:s - 128 + n], in0=T[:, 0:n], in1=hi[:, dch, s:s + n], op=ALU.subtract),
                           r=[b_zr[zi], b_hip[dch][pi]], w=[b_lo[dch], b_lop[dch][pi]])
                    else:
                        opend.append((T, zi, s, n, dch))
                        if len(opend) > 1:
                            emit_out(opend.pop(0))
            while last and opend:
                emit_out(opend.pop(0))

        _STOPPED[0] = False
        allb = (b_hi + b_lo + b_mix + b_kT + b_v + b_w + b_q + [b_scrA, b_scrB] + b_zr + b_Sp + b_sqb + b_st + b_ps
                + b_xst + b_ost8 + [b_const, b_qs, b_usx, b_ysx, b_ucol, b_out] + b_vnew + b_vsbf2 + b_ksT2 + b_sst2 + b_PTs2)
        fw.final_wait("sp", allb)
        fw.replay(block)
    return nc


def _win_col_order():
    ATT, KV, CONV = 1024, 256, 1024
    q0, k0, v0, ga0 = 0, ATT, ATT + KV, ATT + 2 * KV
    b0 = ga0 + ATT
    cc0 = b0 + CONV
    h0 = cc0 + CONV
    gc0 = h0 + CONV
    cols = []
    cols += list(range(v0, v0 + 256))
    cols += list(range(k0, k0 + 256))
    def G(c):
        a, b = head_of(c, 0), head_of(c, 1)
        return list(range(ga0 + a * 64, ga0 + a * 64 + 64)) + list(range(ga0 + b * 64, ga0 + b * 64 + 64))
    def Q(c):
        a, b = head_of(c, 0), head_of(c, 1)
        return list(range(q0 + a * 64, q0 + a * 64 + 64)) + list(range(q0 + b * 64, q0 + b * 64 + 64))
    cols += G(0) + Q(0)
    for c in range(8):
        j = c
        cols += list(range(cc0 + j * 128, cc0 + (j + 1) * 128))
        cols += list(range(h0 + j * 128, h0 + (j + 1) * 128))
        if c + 1 < 8:
            cols += G(c + 1) + Q(c + 1)
        cols += list(range(b0 + j * 128, b0 + (j + 1) * 128))
        cols += list(range(gc0 + j * 128, gc0 + (j + 1) * 128))
    assert len(cols) == NWIN * 128
    return np.array(cols)


def _mix_row_order():
    rows = []
    for c in range(8):
        a, b = head_of(c, 0), head_of(c, 1)
        rows += list(range(a * 64, a * 64 + 64)) + list(range(b * 64, b * 64 + 64))
    rows += list(range(1024, 2048))
    return np.array(rows)


_CACHE = {}


def kernel(x_prompt, x_sample, cache_k, cache_v, state_conv, meta_tokens,
           w_in, conv_w, sinks, w_out, ln_g, ln_b):
    f32 = np.float32
    x_prompt = np.asarray(x_prompt, f32); x_sample = np.asarray(x_sample, f32)
    cache_k = np.asarray(cache_k, f32); cache_v = np.asarray(cache_v, f32)
    state_conv = np.asarray(state_conv, f32); meta_tokens = np.asarray(meta_tokens, f32)
    w_in = np.asarray(w_in, f32); conv_w = np.asarray(conv_w, f32); sinks = np.asarray(sinks, f32)
    w_out = np.asarray(w_out, f32); ln_g = np.asarray(ln_g, f32); ln_b = np.asarray(ln_b, f32)

    if "nc" not in _CACHE:
        _CACHE["nc"] = build_program()
    nc = _CACHE["nc"]

    corder = _win_col_order()
    win = w_in[:, :, corder].reshape(DEPTH, 16, 128, NWIN, 128).transpose(0, 3, 2, 1, 4)
    win = np.ascontiguousarray(win).reshape(DEPTH * NWIN, 128, 2048)
    rorder = _mix_row_order()
    wout = w_out[:, rorder, :].reshape(DEPTH, 16, 128, 16, 128).transpose(0, 3, 2, 1, 4)
    wout = np.ascontiguousarray(wout).reshape(DEPTH * 16, 128, 2048)
    lng = np.ascontiguousarray(ln_g.reshape(DEPTH, 16, 128).transpose(2, 0, 1).reshape(128, 64))
    lnb = np.ascontiguousarray(ln_b.reshape(DEPTH, 16, 128).transpose(2, 0, 1).reshape(128, 64))
    cw = np.ascontiguousarray(conv_w.reshape(DEPTH, 3, 8, 128).transpose(3, 0, 1, 2).reshape(128, 96))
    sperm = np.array([[head_of(c, e) for c in range(8) for e in range(2)]]).reshape(-1)
    sink_bc = np.ascontiguousarray(np.broadcast_to(sinks[:, sperm].reshape(1, 64), (128, 64)))
    srow = sinks.reshape(DEPTH, 4, 4).transpose(2, 0, 1)
    sink_row = np.ascontiguousarray(np.broadcast_to(srow[:, None], (4, 4, DEPTH, 4)).reshape(16, 16))
    ident = np.eye(128, dtype=f32)
    qi = np.arange(128)[:, None]; si = np.arange(256)[None, :]
    dist = 128 + qi - si
    dgen = np.where((dist >= 0) & (dist < 128), -dist.astype(f32), f32(NEG)).astype(f32)
    ti = np.arange(4)[:, None]; ri = np.arange(132)[None, :]
    sdist = 128 + ti - ri
    svalid = (sdist >= 0) & (sdist < 128)
    dms = np.zeros((16, 4, 132), f32)
    for g in range(4):
        for gi in range(4):
            sl = f32(slope(4 * g + gi))
            dms[gi * 4:(gi + 1) * 4, g, :] = np.where(svalid, -sl * sdist.astype(f32), f32(-3.0e6))
    dms = dms.reshape(16, 4 * 132)

    in_maps = []
    for core in range(8):
        b, j = divmod(core, 4)
        xin = np.zeros((NP, D), f32)
        cm = np.ones((128, 386), f32)
        dmt = np.broadcast_to(dgen[:, None, :], (128, 5, 256)).copy()
        if j == 0:
            xin[496:512] = meta_tokens
            xin[512:] = x_prompt[b, 0:1024]
            cm[:, 0:370] = 0.0
            for bi in range(1, 5):
                kcols = np.arange((bi - 1) * 128, (bi + 1) * 128)
                dmt[:, bi, kcols < 496] = NEG
        else:
            xin[:] = x_prompt[b, j * 1024 - 512:(j + 1) * 1024]
        sb0 = core * 4
        in_maps.append({
            "xin": xin,
            "xs": np.ascontiguousarray(x_sample[sb0:sb0 + 4].reshape(NS, D)),
            "ck": np.ascontiguousarray(cache_k[:, sb0:sb0 + 4].reshape(DEPTH, 4, 128, 256)),
            "cv": np.ascontiguousarray(cache_v[:, sb0:sb0 + 4].reshape(DEPTH, 4, 128, 256)),
            "sc": np.ascontiguousarray(state_conv[:, sb0:sb0 + 4].reshape(DEPTH, 8, 1024)),
            "win": win, "wout": wout, "lng": lng, "lnb": lnb, "cw": cw,
            "sinkbc": sink_bc, "sinkrow": sink_row,
            "dm": np.ascontiguousarray(dmt.reshape(128, 5 * 256)), "dms": dms,
            "cmask": cm, "ident": ident,
        })

    ncores = int(os.environ.get("KCORES", "8"))
    res = run_bass_kernel_spmd(nc, in_maps[:ncores], core_ids=list(range(ncores)))
    R = list(res.results) + [res.results[0]] * (8 - ncores)

    y_prompt = np.zeros((2, 4096, D), f32)
    y_sample = np.zeros((32, 4, D), f32)
    kwp = np.zeros((DEPTH, 2, 128, 4, 64), f32); vwp = np.zeros_like(kwp)
    cvp = np.zeros((DEPTH, 2, 2, 1024), f32)
    kws = np.zeros((DEPTH, 32, 128, 4, 64), f32); vws = np.zeros_like(kws)
    cvs = np.zeros((DEPTH, 32, 2, 1024), f32)
    for core in range(8):
        b, j = divmod(core, 4)
        r = R[core]
        y_prompt[b, j * 1024:(j + 1) * 1024] = r["y"]
        sb0 = core * 4
        y_sample[sb0:sb0 + 4] = r["ys"].reshape(4, 4, D)
        kws[:, sb0:sb0 + 4] = r["kws"].reshape(DEPTH, 4, 128, 4, 64)
        vws[:, sb0:sb0 + 4] = r["vws"].reshape(DEPTH, 4, 128, 4, 64)
        cvs[:, sb0:sb0 + 4] = r["convs"].reshape(DEPTH, 4, 2, 1024)
        if j == 3:
            kwp[:, b] = r["kwp"].reshape(DEPTH, 128, 4, 64)
            vwp[:, b] = r["vwp"].reshape(DEPTH, 128, 4, 64)
            cvp[:, b] = r["convp"]
    return (y_prompt, y_sample, kwp, vwp, cvp, kws, vws, cvs)
```

```python
import os
import numpy as np
from contextlib import ExitStack
import concourse.bass as bass
import concourse.mybir as mybir
from concourse.bass_utils import run_bass_kernel_spmd

F32 = mybir.dt.float32
BF16 = mybir.dt.bfloat16
AF = mybir.ActivationFunctionType
ALU = mybir.AluOpType
AX = mybir.AxisListType

D = 2048
DEPTH = 4
NPB = 12
NP = NPB * 128
NS = 16
NT = NP + NS
MC = NT - 128
ALPHA = float((2 * DEPTH) ** 0.25)
EPS = 1e-5
NWIN = 52
NEG = -1.0e9


STOP = float(os.environ.get("KSTOP", "99"))


class _Stop(Exception):
    pass


_STOPPED = [False]


def ckpt(level):
    if STOP <= level:
        _STOPPED[0] = True


class Buf:
    __slots__ = ("w", "r")

    def __init__(self):
        self.w = {}
        self.r = {}


class Eng:
    def __init__(self, name):
        self.name = name
        self.sem = name
        self.count = 0
        self.q = []
        self.waited = {}


class FW:
    def __init__(self, sems, n_dma_sems=10):
        it = iter(sems)
        self.semh = {}
        self.engs = {}
        for name in ("pe", "act", "dve", "pool", "sp"):
            self.semh[name] = next(it)
            self.engs[name] = Eng(name)
        self.dma_ring = {}
        for qn in ("sp", "pool", "act"):
            ring = []
            for i in range(n_dma_sems):
                key = "dma_%s_%d" % (qn, i)
                self.semh[key] = next(it)
                ring.append([key, 0])
            self.dma_ring[qn] = [ring, 0]

    def _collect(self, eng, reads, writes):
        need = {}
        own = eng.sem
        own_skip = (eng.name == "pe")
        for b in reads:
            for k, v in b.w.items():
                if k == own and (own_skip or v > eng.count):
                    continue
                if need.get(k, 0) < v:
                    need[k] = v
        for b in writes:
            for k, v in b.w.items():
                if k == own and (own_skip or v > eng.count):
                    continue
                if need.get(k, 0) < v:
                    need[k] = v
            for k, v in b.r.items():
                if k == own:
                    continue
                if need.get(k, 0) < v:
                    need[k] = v
        waits = []
        for k, v in need.items():
            if eng.waited.get(k, 0) < v:
                eng.waited[k] = v
                waits.append((k, v))
        return waits

    def op(self, engname, fn, reads=(), writes=(), signal=True):
        eng = self.engs[engname]
        waits = self._collect(eng, reads, writes)
        if signal:
            eng.count += 1
            tok = (eng.sem, eng.count)
            eng.q.append((waits, fn, (eng.sem, 1)))
        else:
            tok = (eng.sem, eng.count + 1)
            eng.q.append((waits, fn, None))
        for b in reads:
            if b.r.get(tok[0], 0) < tok[1]:
                b.r[tok[0]] = tok[1]
        for b in writes:
            if b.w.get(tok[0], 0) < tok[1]:
                b.w[tok[0]] = tok[1]
        return tok

    def dma(self, qname, fn, reads=(), writes=()):
        eng = self.engs[qname]
        ringinfo = self.dma_ring[qname]
        ring, idx = ringinfo
        slot = ring[idx]
        ringinfo[1] = (idx + 1) % len(ring)
        key, total = slot
        waits = self._collect(eng, reads, writes)
        if total > 0 and eng.waited.get(key, 0) < total:
            eng.waited[key] = total
            waits.append((key, total))
        slot[1] = total + 16
        tok = (key, total + 16)
        eng.q.append((waits, fn, (key, 16)))
        for b in reads:
            if b.r.get(tok[0], 0) < tok[1]:
                b.r[tok[0]] = tok[1]
        for b in writes:
            if b.w.get(tok[0], 0) < tok[1]:
                b.w[tok[0]] = tok[1]
        return tok

    def final_wait(self, engname, bufs):
        eng = self.engs[engname]
        need = {}
        for b in bufs:
            for d in (b.w, b.r):
                for k, v in d.items():
                    if need.get(k, 0) < v:
                        need[k] = v
        waits = [(k, v) for k, v in need.items() if eng.waited.get(k, 0) < v]
        for k, v in waits:
            eng.waited[k] = v
        eng.q.append((waits, None, None))

    def replay(self, block):
        semh = self.semh

        def run(engname, h):
            for waits, fn, inc in self.engs[engname].q:
                for k, v in waits:
                    h.wait_ge(semh[k], v)
                if fn is not None:
                    ins = fn(h)
                    if inc is not None:
                        ins.then_inc(semh[inc[0]], inc[1])

        @block.tensor
        def _(h):
            run("pe", h)

        @block.scalar
        def _(h):
            run("act", h)

        @block.vector
        def _(h):
            run("dve", h)

        @block.gpsimd
        def _(h):
            run("pool", h)

        @block.sync
        def _(h):
            run("sp", h)


def pieces(c0, c1):
    out = []
    s = c0
    while s < c1:
        n = min(512, c1 - s)
        out.append((s, n))
        s += n
    return out


def head_of(c, e):
    kc2, gi = divmod(c, 4)
    return 4 * (2 * kc2 + e) + gi


def slope(h):
    return float(2.0 ** (-8.0 * (h + 1) / 16.0))


def build_program():
    nc = bass.Bass("TRN2", target_bir_lowering=False)

    def din(name, shape):
        return nc.dram_tensor(name, list(shape), F32, kind="ExternalInput").ap()

    def dout(name, shape):
        return nc.dram_tensor(name, list(shape), F32, kind="ExternalOutput").ap()

    xin = din("xin", [NP, D])
    xsd = din("xs", [NS, D])
    ckd = din("ck", [DEPTH, 4, 128, 256])
    cvd = din("cv", [DEPTH, 4, 128, 256])
    scd = din("sc", [DEPTH, 8, 1024])
    wind = din("win", [DEPTH * NWIN, 128, 2048])
    woutd = din("wout", [DEPTH * 16, 128, 2048])
    lngd = din("lng", [128, 64])
    lnbd = din("lnb", [128, 64])
    cwd = din("cw", [128, 96])
    sinkd = din("sinkbc", [128, 64])
    sinkrd = din("sinkrow", [16, 16])
    dmd = din("dm", [128, 5 * 256])
    dmsd = din("dms", [16, 4 * 132])
    cmd = din("cmask", [128, 386])
    identd = din("ident", [128, 128])

    yd = dout("y", [1024, D])
    ysd = dout("ys", [NS, D])
    kwpd = dout("kwp", [DEPTH, 128, 256])
    vwpd = dout("vwp", [DEPTH, 128, 256])
    cvpd = dout("convp", [DEPTH, 2, 1024])
    kwsd = dout("kws", [DEPTH, 4, 128, 256])
    vwsd = dout("vws", [DEPTH, 4, 128, 256])
    cvsd = dout("convs", [DEPTH, 8, 1024])

    with ExitStack() as es:
        def sb(name, shape, dt):
            return es.enter_context(nc.sbuf_tensor("sb_" + name, list(shape), dt))

        hi = sb("hi", [128, 16, NT], BF16)
        lo = sb("lo", [128, 16, MC], BF16)
        mixT = sb("mixT", [128, 16, MC], BF16)
        kT = sb("kT", [128, 2, NT], BF16)
        vv = sb("vv", [128, NPB, 256], BF16)
        wbuf = [sb("wbuf%d" % i, [128, 16, 128], BF16) for i in range(3)]
        qbuf = [sb("qbuf%d" % i, [128, MC], BF16) for i in range(2)]
        SCRW = 1432
        scr = sb("scr", [128, 2 * SCRW + 3 * 512], F32)
        scrA = scr[:, 0:SCRW]
        scrB = scr[:, SCRW:2 * SCRW]
        zr = [scr[:, 2 * SCRW + i * 512: 2 * SCRW + (i + 1) * 512] for i in range(3)]
        mixf = mixT[:, :, :].rearrange("p c m -> p (c m)").bitcast(F32)
        xst = [mixf[:, k * 2048:(k + 1) * 2048] for k in range(5)]
        ostage8 = [mixf[:, k * 512:(k + 1) * 512].rearrange("p (j f) -> p j f", j=4) for k in range(8)]
        scrA16 = scr[:, 0:SCRW].bitcast(BF16)
        MH = scrA16[:, 0:SCRW]
        ML = scrA16[:, SCRW:2 * SCRW]
        Spf = [sb("Sp%d" % i, [128, 536], F32) for i in range(3)]
        Sp = [t[:, 0:514].rearrange("p (e s) -> p e s", e=2) for t in Spf]
        Spb = [t[:, 0:514].bitcast(BF16).rearrange("p (e s) -> p e s", e=2) for t in Spf]
        PTsbf = [sb("PTsb%d" % i, [128, 512], BF16) for i in range(2)]
        PTsb = [t[:, :].rearrange("p (e s q) -> p e s q", e=2, s=2) for t in PTsbf]
        sqb = PTsbf + [sb("sqb2", [128, 512], BF16)]
        st_negm = [sb("negm%d" % i, [128, 2], F32) for i in range(3)]
        st_den = [sb("den%d" % i, [128, 2], F32) for i in range(3)]
        st_rden = [sb("rden%d" % i, [128, 2], F32) for i in range(3)]
        idf = sb("idf", [128, 128], F32)
        ones_bf = sb("ones_bf", [128, 128], BF16)
        identb = sb("identb", [128, 128], BF16)
        epsb = sb("epsb", [128, 1], F32)
        nidentb = sb("nidentb", [128, 128], BF16)
        dm = sb("dm", [128, 5, 256], BF16)
        dms = sb("dms", [16, 4, 132], F32)
        sinkbc = sb("sinkbc", [128, 64], F32)
        sinkrow = sb("sinkrow", [16, 16], F32)
        lng = sb("lng", [128, 64], F32)
        lnb = sb("lnb", [128, 64], F32)
        cw = sb("cw", [128, 96], F32)
        cmask = sb("cmask", [128, 386], BF16)
        qs = sb("qs", [128, 4, 2, 16], BF16)
        Ss1 = sb("Ss1", [16, 4, 133], F32)
        s_negm2 = sb("s_negm2", [16, 8], F32)
        s_den2 = sb("s_den2", [16, 8], F32)
        s_rden2 = sb("s_rden2", [16, 8], F32)
        PTs2 = sb("PTs2", [128, 2, 4, 16], BF16)
        PT22 = sb("PT22", [4, 2, 4, 16], BF16)
        vnew_bf_t = sb("vnew_bf", [4, 4, 256], BF16)
        vnew_bf = [vnew_bf_t[:, i, :] for i in range(4)]
        usx = sb("usx", [128, 8, 4, 6], F32)
        ysx = sb("ysx", [128, 4, 4], F32)
        ucol = sb("ucol", [128, 8, 10], F32)

        kst = zr[0][:, 0:256]
        vst = zr[0][:, 256:512]
        ostage = [t[:, 0:512].rearrange("p (j f) -> p j f", j=4) for t in Spf]
        ps = [es.enter_context(nc.psum_tensor("ps%d" % i, [128, 512], F32)) for i in range(8)]
        ps6b = ps[6][:, :].bitcast(BF16)
        sems = [es.enter_context(nc.semaphore("s%d" % i)) for i in range(5 + 30)]
        fw = FW(sems, n_dma_sems=10)
        if os.environ.get('KVERBOSE'):
            print('sbuf remaining', nc.sbuf_bytes_remaining)
        block = es.enter_context(nc.Block())

        b_hi = [Buf() for _ in range(16)]
        b_lo = [Buf() for _ in range(16)]
        b_hip = [[Buf() for _ in range(4)] for _ in range(16)]
        b_lop = [[Buf() for _ in range(4)] for _ in range(16)]
        b_mix = [Buf() for _ in range(16)]
        b_kT = [Buf() for _ in range(2)]
        b_v = [Buf() for _ in range(NPB)]
        b_w = [Buf() for _ in range(3)]
        b_q = [Buf() for _ in range(2)]
        b_scrA = Buf()
        b_scrB = Buf()
        b_zr = [Buf() for _ in range(3)]
        b_Sp = [Buf() for _ in range(3)]
        b_PTsb = [Buf() for _ in range(2)]
        b_sqb = b_PTsb + [Buf()]
        b_st = [Buf() for _ in range(3)]
        b_ps = [Buf() for _ in range(8)]
        b_const = Buf()
        b_vsbf = Buf(); b_ksT = Buf(); b_qs = Buf(); b_Ss = b_Sp[2]
        b_sst = Buf(); b_PTs = Buf(); b_vnew = [Buf() for _ in range(4)]
        b_usx = Buf(); b_ysx = Buf(); b_ucol = Buf()
        b_ost = b_Sp
        b_xst = [Buf() for _ in range(5)]
        b_ost8 = [Buf() for _ in range(8)]
        b_kst = b_zr[0]; b_vst = b_zr[0]
        b_out = Buf()
        b_Ss1 = Buf()
        b_vsbf2 = [Buf(), Buf()]; b_ksT2 = [Buf(), Buf()]; b_sst2 = [Buf(), Buf()]; b_PTs2 = [Buf(), Buf()]

        _STOPPED[0] = False

        def OP(eng, fn, r=(), w=(), sig=True):
            if _STOPPED[0]:
                return None
            return fw.op(eng, fn, reads=r, writes=w, signal=sig)

        def DMA(q, fn, r=(), w=()):
            if _STOPPED[0]:
                return None
            return fw.dma(q, fn, reads=r, writes=w)

        b_idf = Buf()
        DMA("act", lambda h: h.dma_start(out=idf[:], in_=identd[:, :]), w=[b_const, b_idf])
        for (t, d) in ((sinkbc, sinkd), (sinkrow, sinkrd), (lng, lngd), (lnb, lnbd),
                       (cw, cwd)):
            DMA("act", lambda h, t=t, d=d: h.dma_start(out=t[:], in_=d[:, :]), w=[b_const])
        DMA("pool", lambda h: h.dma_start(out=dm[:], in_=dmd.rearrange("p (t s) -> p t s", t=5)), w=[b_const])
        DMA("pool", lambda h: h.dma_start(out=cmask[:], in_=cmd[:, :]), w=[b_const])
        DMA("act", lambda h: h.dma_start(out=dms[:], in_=dmsd.rearrange("p (g s) -> p g s", g=4)), w=[b_const])
        OP("dve", lambda h: h.memset(ones_bf[:], 1.0), w=[b_const])
        OP("dve", lambda h: h.memset(epsb[:], EPS), w=[b_const])
        OP("dve", lambda h: h.tensor_copy(out=identb[:], in_=idf[:]), r=[b_const], w=[b_const])
        OP("dve", lambda h: h.tensor_scalar(out=nidentb[:], in0=idf[:], scalar1=-1.0, scalar2=None, op0=ALU.mult), r=[b_const], w=[b_const])

        wlist = []
        for l in range(DEPTH):
            for i in range(NWIN):
                wlist.append(("in", l * NWIN + i))
            for dch in range(16):
                wlist.append(("out", l * 16 + dch))
        wstate = {"issued": 0, "used": 0}

        def issue_w(upto):
            while wstate["issued"] < min(upto, len(wlist)):
                n = wstate["issued"]
                kind, idx = wlist[n]
                src = (wind if kind == "in" else woutd)[idx].rearrange("p (k c) -> p k c", k=16)
                slot = n % 3
                DMA("pool", lambda h, slot=slot, src=src: h.dma_start(out=wbuf[slot][:], in_=src), w=[b_w[slot]])
                wstate["issued"] += 1

        def next_w(ahead=2):
            n = wstate["used"]
            wstate["used"] += 1
            issue_w(n + 1 + ahead)
            return n % 3

        OB = (3, 7)
        bank_state = {"n": 0}

        def next_bank(nring=3):
            b = bank_state["n"] % nring
            bank_state["n"] += 1
            return b

        issue_w(2)

        def emit_V_block(blk, l_, vslots):
            bank = next_bank()
            for half in range(2):
                for kc in range(16):
                    OP("pe", lambda h, half=half, kc=kc, bank=bank, vs=vslots[half]: h.matmul(
                        ps[bank][:, half * 128:(half + 1) * 128], lhsT=hi[:, kc, blk * 128:(blk + 1) * 128],
                        rhs=wbuf[vs][:, kc, :], start=(kc == 0), stop=(kc == 15)),
                       r=[b_w[vslots[half]], b_hi[kc]], w=[b_ps[bank]], sig=(kc == 15 and half == 1))
            OP("dve", lambda h: h.tensor_copy(out=vv[:, blk, :], in_=ps[bank][:, 0:256]),
               r=[b_ps[bank]], w=[b_v[blk]])
            if blk == NPB - 1:
                OP("act", lambda h: h.copy(out=zr[2][:, 256:512], in_=ps[bank][:, 0:256]),
                   r=[b_ps[bank], b_v[blk]], w=[b_zr[2]])
                DMA("sp", lambda h: h.dma_start(out=vwpd[l_], in_=zr[2][:, 256:512]), r=[b_zr[2]], w=[b_out])

        vslots0 = [next_w(2), next_w(1)]

        def issue_x(blk):
            st = xst[blk % 5]
            bx = b_xst[blk % 5]
            if blk < NPB:
                DMA("sp", lambda h, st=st, blk=blk: h.dma_start(out=st[:, :], in_=xin[blk * 128:(blk + 1) * 128, :]), w=[bx])
            else:
                DMA("sp", lambda h, st=st: h.dma_start(out=st[0:NS, :], in_=xsd[:, :]), w=[bx])

        for blk in range(5):
            issue_x(blk)
        for blk in range(NPB + 1):
            st = xst[blk % 5]
            bx = b_xst[blk % 5]
            nrow = 128 if blk < NPB else NS
            c_lo = blk * 128
            for g in range(4):
                bank = next_bank()
                for j in range(4):
                    kc = g * 4 + j
                    OP("pe", lambda h, st=st, kc=kc, j=j, bank=bank, nrow=nrow: h.transpose(
                        out=ps[bank][:, j * 128:j * 128 + nrow], in_=st[0:nrow, kc * 128:(kc + 1) * 128],
                        identity=idf[0:nrow, 0:nrow]),
                       r=[bx, b_idf], w=[b_ps[bank]], sig=(j == 3))
                src = ps[bank][:, :].rearrange("p (j t) -> p j t", j=4)[:, :, 0:nrow]
                OP("act", lambda h, g=g, src=src, c_lo=c_lo, nrow=nrow: h.copy(
                    out=hi[:, g * 4:(g + 1) * 4, c_lo:c_lo + nrow], in_=src),
                   r=[b_ps[bank]], w=b_hi[g * 4:(g + 1) * 4])
                if blk > 0:
                    OP("dve", lambda h, g=g, src=src, c_lo=c_lo, nrow=nrow: h.tensor_tensor(
                        out=lo[:, g * 4:(g + 1) * 4, c_lo - 128:c_lo - 128 + nrow], in0=src,
                        in1=hi[:, g * 4:(g + 1) * 4, c_lo:c_lo + nrow], op=ALU.subtract),
                       r=[b_ps[bank]] + b_hi[g * 4:(g + 1) * 4], w=b_lo[g * 4:(g + 1) * 4])
            if blk + 5 < NPB + 1:
                issue_x(blk + 5)
            if 1 <= blk <= NPB:
                emit_V_block(blk - 1, 0, vslots0)

        def proj_piece(slot, s, n, bank):
            for kc in range(16):
                OP("pe", lambda h, slot=slot, kc=kc, s=s, n=n, bank=bank: h.matmul(
                    ps[bank][:, 0:n], lhsT=wbuf[slot][:, kc, :], rhs=hi[:, kc, s:s + n],
                    start=(kc == 0), stop=(kc == 15)),
                   r=[b_w[slot], b_hi[kc]], w=[b_ps[bank]], sig=(kc == 15))

        ckpt(1)
        for l in range(DEPTH):
            c0 = 128 * (l + 1)
            k0 = 128 * l
            full = pieces(c0, NT)
            last = (l == DEPTH - 1)

            for hf in range(2):
                DMA("sp", lambda h, l=l, hf=hf: h.dma_start(out=zr[hf][0:8, :], in_=scd[l][:, hf * 512:(hf + 1) * 512]), w=[b_zr[hf]])
            bank = next_bank()
            for j in range(8):
                OP("pe", lambda h, j=j, bank=bank: h.transpose(
                    out=ps[bank][:, j * 8:(j + 1) * 8], in_=zr[j // 4][0:8, (j % 4) * 128:(j % 4 + 1) * 128], identity=idf[0:8, 0:8]),
                   r=[b_zr[j // 4], b_const], w=[b_ps[bank]], sig=(j == 7))
            OP("act", lambda h, bank=bank: h.copy(
                out=usx[:, :, :, 0:2], in_=ps[bank][:, 0:64].rearrange("p (j b t) -> p j b t", j=8, b=4)),
               r=[b_ps[bank]], w=[b_usx])

            if l == 0:
                vslots = vslots0
            else:
                vslots = [next_w(2), next_w(1)]
                for blk in range(l, NPB):
                    emit_V_block(blk, l, vslots)
            for bb in range(4):
                bank = next_bank()
                for half in range(2):
                    for kc in range(16):
                        OP("pe", lambda h, half=half, kc=kc, bank=bank, bb=bb, vs=vslots[half]: h.matmul(
                            ps[bank][0:4, half * 128:(half + 1) * 128], lhsT=hi[:, kc, NP + bb * 4:NP + bb * 4 + 4],
                            rhs=wbuf[vs][:, kc, :], start=(kc == 0), stop=(kc == 15)),
                           r=[b_w[vslots[half]], b_hi[kc]], w=[b_ps[bank]], sig=(kc == 15 and half == 1))
                OP("dve", lambda h, bank=bank, bb=bb: h.tensor_copy(out=vnew_bf[bb], in_=ps[bank][0:4, 0:256]),
                   r=[b_ps[bank]], w=[b_vnew[bb]])
                OP("act", lambda h, bank=bank, bb=bb: h.copy(out=zr[bb % 2][0:4, 0:256], in_=ps[bank][0:4, 0:256]),
                   r=[b_ps[bank], b_vnew[bb]], w=[b_zr[bb % 2]])
                DMA("sp", lambda h, l=l, bb=bb: h.dma_start(out=vwsd[l, bb, 124:128, :], in_=zr[bb % 2][0:4, 0:256]),
                    r=[b_zr[bb % 2]], w=[b_out])

            for kc2 in range(2):
                slot = next_w()
                for (s, n) in [(k0, 128)] + full:
                    bank = next_bank()
                    proj_piece(slot, s, n, bank)
                    OP("act", lambda h, kc2=kc2, s=s, n=n, bank=bank: h.copy(out=kT[:, kc2, s:s + n], in_=ps[bank][:, 0:n]),
                       r=[b_ps[bank]], w=[b_kT[kc2]])
                bank = next_bank()
                for kc in range(16):
                    OP("pe", lambda h, slot=slot, kc=kc, bank=bank: h.matmul(
                        ps[bank][:, 0:128], lhsT=hi[:, kc, NP - 128:NP], rhs=wbuf[slot][:, kc, :],
                        start=(kc == 0), stop=(kc == 15)),
                       r=[b_w[slot], b_hi[kc]], w=[b_ps[bank]], sig=(kc == 15))
                OP("act", lambda h, kc2=kc2, bank=bank: h.copy(out=zr[2][:, kc2 * 128:(kc2 + 1) * 128], in_=ps[bank][:, 0:128]),
                   r=[b_ps[bank]], w=[b_zr[2]])
                bank = next_bank()
                for bb in range(4):
                    for kc in range(16):
                        OP("pe", lambda h, slot=slot, kc=kc, bank=bank, bb=bb: h.matmul(
                            ps[bank][0:4, bb * 128:(bb + 1) * 128], lhsT=hi[:, kc, NP + bb * 4:NP + bb * 4 + 4],
                            rhs=wbuf[slot][:, kc, :], start=(kc == 0), stop=(kc == 15)),
                           r=[b_w[slot], b_hi[kc]], w=[b_ps[bank]], sig=(kc == 15 and bb == 3))
                OP("dve", lambda h, kc2=kc2, bank=bank: h.tensor_copy(out=zr[kc2][0:4, :], in_=ps[bank][0:4, :]),
                   r=[b_ps[bank]], w=[b_zr[kc2]])
                DMA("sp", lambda h, l=l, kc2=kc2: h.dma_start(
                    out=kwsd[l, :, 124:128, kc2 * 128:(kc2 + 1) * 128].rearrange("b t f -> t b f"),
                    in_=zr[kc2][0:4, :].rearrange("p (b c) -> p b c", b=4)), r=[b_zr[kc2]], w=[b_out])
            DMA("sp", lambda h, l=l: h.dma_start(out=kwpd[l], in_=zr[2][:, 0:256]), r=[b_zr[2]], w=[b_out])
            DMA("sp", lambda h, l=l: h.dma_start(out=kwsd[l, :, 0:124, :], in_=ckd[l, :, 4:128, :]), w=[b_out])
            DMA("sp", lambda h, l=l: h.dma_start(out=vwsd[l, :, 0:124, :], in_=cvd[l, :, 4:128, :]), w=[b_out])

            ckpt(2 + 10 * l)
            ckpt(3 + 10 * l)
            zcount = [0]

            def gen_G(c):
                slot = next_w()
                for (s, n) in full:
                    bank = next_bank()
                    proj_piece(slot, s, n, bank)
                    zi = zcount[0] % 2
                    zcount[0] += 1
                    OP("act", lambda h, n=n, bank=bank, zi=zi: h.activation(
                        out=zr[zi][:, 0:n], in_=ps[bank][:, 0:n], func=AF.Tanh, scale=0.5),
                       r=[b_ps[bank]], w=[b_zr[zi]])
                    OP("dve", lambda h, c=c, s=s, n=n, bank=bank, zi=zi: h.scalar_tensor_tensor(
                        out=mixT[:, c, s - 128:s - 128 + n], in0=zr[zi][:, 0:n], scalar=1.0, in1=ps[bank][:, 0:n],
                        op0=ALU.add, op1=ALU.mult),
                       r=[b_zr[zi], b_ps[bank]], w=[b_mix[c]])
                    yield

            def gen_Q(c):
                slot = next_w()
                qb = c % 2
                for (s, n) in full:
                    bank = next_bank()
                    proj_piece(slot, s, n, bank)
                    OP("act", lambda h, qb=qb, s=s, n=n, bank=bank: h.activation(
                        out=qbuf[qb][:, s - 128:s - 128 + n], in_=ps[bank][:, 0:n], func=AF.Copy, scale=0.125),
                       r=[b_ps[bank]], w=[b_q[qb]])
                    yield
                OP("act", lambda h, qb=qb, c=c: h.copy(
                    out=qs[:, :, c // 4, (c % 4) * 4:(c % 4) * 4 + 4],
                    in_=qbuf[qb][:, MC - NS:MC].rearrange("p (b t) -> p b t", b=4)),
                   r=[b_q[qb]], w=[b_qs])

            def gen_convA(j):
                cbase = l * 24 + j
                slot = next_w()
                for (s, n) in pieces(c0 - 2, NT):
                    bank = next_bank()
                    proj_piece(slot, s, n, bank)
                    OP("act", lambda h, s=s, n=n, bank=bank: h.copy(out=scrA[:, s - 126:s - 126 + n], in_=ps[bank][:, 0:n]),
                       r=[b_ps[bank]], w=[b_scrA])
                    yield
                slot = next_w()
                for (s, n) in pieces(c0 - 2, NT):
                    bank = next_bank()
                    proj_piece(slot, s, n, bank)
                    OP("dve", lambda h, s=s, n=n, bank=bank: h.tensor_tensor(
                        out=scrB[:, s - 126:s - 126 + n], in0=scrA[:, s - 126:s - 126 + n], in1=ps[bank][:, 0:n], op=ALU.mult),
                       r=[b_ps[bank], b_scrA], w=[b_scrB])
                    yield
                OP("dve", lambda h, c0=c0: h.tensor_tensor(
                    out=scrB[:, c0 - 128:512 - 126], in0=scrB[:, c0 - 128:512 - 126], in1=cmask[:, c0 - 128:386], op=ALU.mult),
                   r=[b_const], w=[b_scrB])
                npc = NP - c0
                a0 = c0 - 126
                OP("act", lambda h: h.activation(out=scrA[:, a0:a0 + npc], in_=scrB[:, a0 - 2:a0 - 2 + npc],
                                                 func=AF.Identity, scale=cw[:, cbase:cbase + 1]),
                   r=[b_scrB, b_const], w=[b_scrA])
                OP("dve", lambda h: h.scalar_tensor_tensor(
                    out=scrA[:, a0:a0 + npc], in0=scrB[:, a0 - 1:a0 - 1 + npc], scalar=cw[:, cbase + 8:cbase + 9],
                    in1=scrA[:, a0:a0 + npc], op0=ALU.mult, op1=ALU.add),
                   r=[b_scrB, b_const], w=[b_scrA])
                OP("dve", lambda h: h.scalar_tensor_tensor(
                    out=scrA[:, a0:a0 + npc], in0=scrB[:, a0:a0 + npc], scalar=cw[:, cbase + 16:cbase + 17],
                    in1=scrA[:, a0:a0 + npc], op0=ALU.mult, op1=ALU.add),
                   r=[b_scrB, b_const], w=[b_scrA])
                us = NP - 126
                OP("dve", lambda h: h.tensor_copy(out=usx[:, j, :, 2:6], in_=scrB[:, us:us + NS].rearrange("p (b t) -> p b t", b=4)),
                   r=[b_scrB], w=[b_usx])
                OP("dve", lambda h: h.tensor_scalar(out=ysx[:, :, :], in0=usx[:, j, :, 0:4], scalar1=cw[:, cbase:cbase + 1],
                                                     scalar2=None, op0=ALU.mult),
                   r=[b_usx, b_const], w=[b_ysx])
                OP("dve", lambda h: h.scalar_tensor_tensor(
                    out=ysx[:, :, :], in0=usx[:, j, :, 1:5], scalar=cw[:, cbase + 8:cbase + 9], in1=ysx[:, :, :],
                    op0=ALU.mult, op1=ALU.add), r=[b_usx, b_const], w=[b_ysx])
                OP("dve", lambda h: h.scalar_tensor_tensor(
                    out=scrA[:, us:us + NS].rearrange("p (b t) -> p b t", b=4), in0=usx[:, j, :, 2:6],
                    scalar=cw[:, cbase + 16:cbase + 17], in1=ysx[:, :, :], op0=ALU.mult, op1=ALU.add),
                   r=[b_usx, b_ysx, b_const], w=[b_scrA])
                OP("dve", lambda h: h.tensor_copy(out=ucol[:, j, 0:2], in_=scrB[:, us - 2:us]), r=[b_scrB], w=[b_ucol])
                OP("dve", lambda h: h.tensor_copy(
                    out=ucol[:, j, 2:10].rearrange("p (b t) -> p b t", b=4),
                    in_=scrB[:, us:us + NS].rearrange("p (b t) -> p b t", b=4)[:, :, 2:4]), r=[b_scrB], w=[b_ucol])

            def gen_convB(j):
                slot = next_w()
                for (s, n) in full:
                    bank = next_bank()
                    proj_piece(slot, s, n, bank)
                    OP("dve", lambda h, s=s, n=n, bank=bank: h.tensor_tensor(
                        out=scrA[:, s - 126:s - 126 + n], in0=ps[bank][:, 0:n], in1=scrA[:, s - 126:s - 126 + n],
                        op=ALU.mult),
                       r=[b_ps[bank]], w=[b_scrA])
                    yield
                slot = next_w()
                for pi, (s, n) in enumerate(full):
                    bank = next_bank()
                    proj_piece(slot, s, n, bank)
                    zi = zcount[0] % 2
                    zcount[0] += 1
                    OP("act", lambda h, n=n, bank=bank, zi=zi: h.activation(
                        out=zr[zi][:, 0:n], in_=ps[bank][:, 0:n], func=AF.Tanh, scale=0.5),
                       r=[b_ps[bank]], w=[b_zr[zi]])
                    OP("dve", lambda h, n=n, bank=bank, zi=zi: h.scalar_tensor_tensor(
                        out=zr[zi][:, 0:n], in0=zr[zi][:, 0:n], scalar=1.0, in1=ps[bank][:, 0:n], op0=ALU.add, op1=ALU.mult),
                       r=[b_ps[bank]], w=[b_zr[zi]])
                    OP("dve", lambda h, s=s, n=n, zi=zi, j=j: h.scalar_tensor_tensor(
                        out=mixT[:, 8 + j, s - 128:s - 128 + n], in0=zr[zi][:, 0:n], scalar=0.5,
                        in1=scrA[:, s - 126:s - 126 + n], op0=ALU.mult, op1=ALU.mult),
                       r=[b_scrA, b_zr[zi]], w=[b_mix[8 + j]])
                    yield

            def chain(*gens):
                for g in gens:
                    for _ in g:
                        yield

            steps = [(c, bi) for c in range(8) for bi in range(l + 1, NPB)]
            nsteps = len(steps)
            per_c = NPB - 1 - l

            def st_S(i):
                c, bi = steps[i]
                kc2 = c // 4
                r = i % 3
                for e in range(2):
                    OP("pe", lambda h, e=e, c=c, bi=bi, kc2=kc2: h.matmul(
                        ps[4 + e][:, 0:256], lhsT=qbuf[c % 2][e * 64:(e + 1) * 64, (bi - 1) * 128:bi * 128],
                        rhs=kT[e * 64:(e + 1) * 64, kc2, (bi - 1) * 128:(bi + 1) * 128], start=True, stop=True),
                       r=[b_q[c % 2], b_kT[kc2]], w=[b_ps[4 + e]], sig=True)
                if i == 0: ckpt(3.21 + 10 * l)
                sk = l * 16 + c * 2
                OP("act", lambda h, r=r, sk=sk: h.copy(out=Sp[r][:, :, 256:257], in_=sinkbc[:, sk:sk + 2].rearrange("p (e o) -> p e o", o=1)),
                   r=[b_const], w=[b_Sp[r]])
                tbl = bi if bi <= 4 else 0
                for e in range(2):
                    sl = slope(head_of(c, e))
                    OP("dve", lambda h, e=e, r=r, tbl=tbl, sl=sl: h.scalar_tensor_tensor(
                        out=Sp[r][:, e, 0:256], in0=dm[:, tbl, :], scalar=sl, in1=ps[4 + e][:, 0:256],
                        op0=ALU.mult, op1=ALU.add),
                       r=[b_ps[4 + e], b_const], w=[b_Sp[r]])
                if i == 0: ckpt(3.22 + 10 * l)
                if i == 0: ckpt(3.23 + 10 * l)
                OP("dve", lambda h, r=r: h.tensor_reduce(out=st_negm[r][:, :], in_=Sp[r][:, :, :], axis=AX.X, op=ALU.max, negate=True),
                   r=[b_Sp[r]], w=[b_st[r]])
                if i == 0: ckpt(3.24 + 10 * l)
                for e in range(2):
                    OP("act", lambda h, e=e, r=r: h.activation(
                        out=Sp[r][:, e, :], in_=Sp[r][:, e, :], func=AF.Exp, bias=st_negm[r][:, e:e + 1], scale=1.0,
                        accum_out=st_den[r][:, e:e + 1]),
                       r=[b_Sp[r], b_st[r]], w=[b_Sp[r], b_st[r]])
                if i == 0: ckpt(3.25 + 10 * l)
                OP("dve", lambda h, r=r: h.reciprocal(out=st_rden[r][:, :], in_=st_den[r][:, :]), r=[b_st[r]], w=[b_st[r]])
                for e in range(2):
                    OP("dve", lambda h, e=e, r=r: h.tensor_scalar(
                        out=Spb[r][:, e, 0:256], in0=Sp[r][:, e, 0:256], scalar1=st_rden[r][:, e:e + 1], scalar2=None,
                        op0=ALU.mult),
                       r=[b_Sp[r], b_st[r]], w=[b_Sp[r]])

            def st_T(i):
                r = i % 3
                rp = i % 2
                for e in range(2):
                    for sc in range(2):
                        OP("pe", lambda h, e=e, sc=sc, r=r: h.transpose(
                            out=ps6b[:, (e * 2 + sc) * 128:(e * 2 + sc + 1) * 128], in_=Spb[r][:, e, sc * 128:(sc + 1) * 128],
                            identity=identb[:, :]),
                           r=[b_Sp[r], b_const], w=[b_ps[6]], sig=(e == 1 and sc == 1))
                OP("act", lambda h, rp=rp: h.copy(out=PTsb[rp][:, :, :, :], in_=ps6b[:, 0:512].rearrange("p (e s q) -> p e s q", e=2, s=2)),
                   r=[b_ps[6]], w=[b_PTsb[rp]])

            def st_PV(i):
                c, bi = steps[i]
                kc2 = c // 4
                r = i % 2
                for e in range(2):
                    for sc in range(2):
                        OP("pe", lambda h, e=e, sc=sc, r=r, bi=bi, kc2=kc2: h.matmul(
                            ps[OB[r]][:, e * 128:(e + 1) * 128],
                            lhsT=vv[:, bi - 1 + sc, kc2 * 128:(kc2 + 1) * 128], rhs=PTsb[r][:, e, sc, :],
                            start=(sc == 0), stop=(sc == 1)),
                           r=[b_v[bi - 1 + sc], b_PTsb[r]], w=[b_ps[OB[r]]], sig=(e == 1 and sc == 1))
                for e in range(2):
                    OP("dve", lambda h, e=e, r=r, c=c, bi=bi: h.scalar_tensor_tensor(
                        out=mixT[e * 64:(e + 1) * 64, c, (bi - 1) * 128:bi * 128],
                        in0=ps[OB[r]][e * 64:(e + 1) * 64, e * 128:(e + 1) * 128], scalar=0.5,
                        in1=mixT[e * 64:(e + 1) * 64, c, (bi - 1) * 128:bi * 128], op0=ALU.mult, op1=ALU.mult),
                       r=[b_ps[OB[r]]], w=[b_mix[c]])

            def sL(bb, l=l):
                rr = bb % 2
                kst_r = zr[2][:, 0:256]
                vst_r = zr[2][:, 256:512]
                vsb = qbuf[0][:, rr * 256:(rr + 1) * 256]
                DMA("sp", lambda h: h.dma_start(out=kst_r, in_=ckd[l, bb]), w=[b_zr[2]])
                DMA("sp", lambda h: h.dma_start(out=vst_r, in_=cvd[l, bb]), w=[b_zr[2]])
                OP("dve", lambda h: h.tensor_copy(out=vsb, in_=vst_r), r=[b_zr[2]], w=[b_vsbf2[rr], b_q[0]])

            def sA(bb, l=l):
                rr = bb % 2
                kst_r = zr[2][:, 0:256]
                vst_r = zr[2][:, 256:512]
                vsb = qbuf[0][:, rr * 256:(rr + 1) * 256]
                ksT_r = qbuf[0][:, 512 + rr * 264:512 + (rr + 1) * 264].rearrange("p (k s) -> p k s", k=2)
                Ss_r = Ss1
                bS = b_Ss1
                ng = s_negm2[:, rr * 4:(rr + 1) * 4]
                dn = s_den2[:, rr * 4:(rr + 1) * 4]
                rd = s_rden2[:, rr * 4:(rr + 1) * 4]
                for kc2 in range(2):
                    OP("pe", lambda h, kc2=kc2: h.transpose(out=ps[6][:, kc2 * 128:(kc2 + 1) * 128],
                                                            in_=kst_r[:, kc2 * 128:(kc2 + 1) * 128], identity=idf[:, :]),
                       r=[b_zr[2], b_const], w=[b_ps[6]], sig=(kc2 == 1))
                OP("act", lambda h: h.copy(out=ksT_r[:, :, 0:128], in_=ps[6][:, 0:256].rearrange("p (k s) -> p k s", k=2)),
                   r=[b_ps[6]], w=[b_ksT2[rr], b_q[0]])
                OP("dve", lambda h: h.tensor_copy(out=ksT_r[:, :, 128:132], in_=kT[:, :, NP + bb * 4:NP + bb * 4 + 4]),
                   r=b_kT, w=[b_ksT2[rr], b_q[0]])
                for g in range(4):
                    kc2, e = divmod(g, 2)
                    bk = 4 + e
                    OP("pe", lambda h, kc2=kc2, e=e, bk=bk: h.matmul(
                        ps[bk][0:16, kc2 * 132:(kc2 + 1) * 132],
                        lhsT=qs[e * 64:(e + 1) * 64, bb, kc2, :],
                        rhs=ksT_r[e * 64:(e + 1) * 64, kc2, :], start=True, stop=True),
                       r=[b_qs, b_ksT2[rr], b_q[0]], w=[b_ps[bk]], sig=True)
                for g in range(4):
                    kc2, e = divmod(g, 2)
                    bk = 4 + e
                    OP("dve", lambda h, g=g, kc2=kc2, bk=bk: h.tensor_tensor(
                        out=Ss_r[:, g, 0:132], in0=ps[bk][0:16, kc2 * 132:(kc2 + 1) * 132],
                        in1=dms[:, g, :], op=ALU.add),
                       r=[b_ps[bk], b_const], w=[bS])
                OP("dve", lambda h: h.tensor_copy(out=Ss_r[:, :, 132:133], in_=sinkrow[:, l * 4:(l + 1) * 4].rearrange("p (g o) -> p g o", o=1)),
                   r=[b_const], w=[bS])
                OP("dve", lambda h: h.tensor_reduce(out=ng, in_=Ss_r[:, :, :], axis=AX.X, op=ALU.max, negate=True),
                   r=[bS], w=[b_sst2[rr]])
                for g in range(4):
                    OP("act", lambda h, g=g: h.activation(out=Ss_r[:, g, :], in_=Ss_r[:, g, :], func=AF.Exp,
                                                          bias=ng[:, g:g + 1], scale=1.0, accum_out=dn[:, g:g + 1]),
                       r=[bS, b_sst2[rr]], w=[bS, b_sst2[rr]])
                OP("dve", lambda h: h.reciprocal(out=rd, in_=dn), r=[b_sst2[rr]], w=[b_sst2[rr]])
                for g in range(4):
                    OP("dve", lambda h, g=g: h.tensor_scalar(out=Ss_r[:, g, 0:132], in0=Ss_r[:, g, 0:132],
                                                             scalar1=rd[:, g:g + 1], scalar2=None, op0=ALU.mult),
                       r=[bS, b_sst2[rr]], w=[bS])

            def sB(bb, l=l):
                rr = bb % 2
                Ss_r = Ss1
                bS = b_Ss1
                for g in range(4):
                    OP("pe", lambda h, g=g: h.transpose(out=ps[6][:, 256 + g * 16:256 + (g + 1) * 16], in_=Ss_r[:, g, 0:128],
                                                        identity=idf[0:16, 0:16]),
                       r=[bS, b_const], w=[b_ps[6]], sig=False)
                    OP("pe", lambda h, g=g: h.transpose(out=ps[6][0:4, 320 + g * 16:320 + (g + 1) * 16], in_=Ss_r[:, g, 128:132],
                                                        identity=idf[0:16, 0:16]),
                       r=[bS, b_const], w=[b_ps[6]], sig=(g == 3))
                OP("act", lambda h: h.copy(out=PTs2[:, rr, :, :], in_=ps[6][:, 256:320].rearrange("p (g q) -> p g q", g=4)),
                   r=[b_ps[6]], w=[b_PTs2[rr]])
                OP("act", lambda h: h.copy(out=PT22[:, rr, :, :], in_=ps[6][0:4, 320:384].rearrange("p (g q) -> p g q", g=4)),
                   r=[b_ps[6]], w=[b_PTs2[rr]])

            def sC(bb, l=l):
                rr = bb % 2
                vsb = qbuf[0][:, rr * 256:(rr + 1) * 256]
                ob = OB[rr]
                for g in range(4):
                    kc2, e = divmod(g, 2)
                    OP("pe", lambda h, g=g, kc2=kc2: h.matmul(
                        ps[ob][:, g * 16:(g + 1) * 16], lhsT=vsb[:, kc2 * 128:(kc2 + 1) * 128], rhs=PTs2[:, rr, g, :],
                        start=True, stop=False), r=[b_vsbf2[rr], b_q[0], b_PTs2[rr]], w=[b_ps[ob]], sig=False)
                    OP("pe", lambda h, g=g, kc2=kc2: h.matmul(
                        ps[ob][:, g * 16:(g + 1) * 16], lhsT=vnew_bf_t[0:4, bb, kc2 * 128:(kc2 + 1) * 128], rhs=PT22[0:4, rr, g, :],
                        start=False, stop=True), r=[b_vnew[bb], b_PTs2[rr]], w=[b_ps[ob]], sig=(g == 3))
                for g in range(4):
                    kc2, e = divmod(g, 2)
                    mc0 = MC - NS + bb * 4
                    OP("dve", lambda h, g=g, kc2=kc2, e=e, mc0=mc0: h.scalar_tensor_tensor(
                        out=mixT[e * 64:(e + 1) * 64, kc2 * 4:(kc2 + 1) * 4, mc0:mc0 + 4],
                        in0=ps[ob][e * 64:(e + 1) * 64, g * 16:(g + 1) * 16].rearrange("p (c t) -> p c t", c=4), scalar=0.5,
                        in1=mixT[e * 64:(e + 1) * 64, kc2 * 4:(kc2 + 1) * 4, mc0:mc0 + 4], op0=ALU.mult, op1=ALU.mult),
                       r=[b_ps[ob]], w=b_mix[kc2 * 4:(kc2 + 1) * 4])

            samp_sched = [[(sL, 0)], [(sA, 0), (sL, 1)], [(sB, 0)], [(sC, 0), (sA, 1), (sL, 2)], [(sB, 1)], [(sC, 1), (sA, 2), (sL, 3)],
                          [(sB, 2)], [(sC, 2), (sA, 3)], [(sB, 3)], [(sC, 3)]]

            for _ in chain(gen_G(0), gen_Q(0)):
                pass
            ckpt(3.1 + 10 * l)
            filler = None
            fill_left = 0
            for i in range(nsteps + 3):
                if i < nsteps:
                    c, bi = steps[i]
                    if bi == l + 1:
                        if filler is not None:
                            for _ in filler:
                                pass
                        gens = [gen_convA(c)]
                        if c + 1 < 8:
                            gens += [gen_G(c + 1), gen_Q(c + 1)]
                        gens.append(gen_convB(c))
                        filler = chain(*gens)
                        fill_left = 3 * (2 if c + 1 < 8 else 0) + 12
                        steps_left = per_c + (3 if c == 7 else 0)
                        fill_tot = fill_left
                        steps_tot = steps_left
                        step_k = 0
                    if c == 1 and bi == l + 1:
                        ckpt(3.6 + 10 * l)
                    st_S(i)
                if steps_left > 0:
                    npull = ((step_k + 1) * fill_tot + steps_tot - 1) // steps_tot - (step_k * fill_tot + steps_tot - 1) // steps_tot
                    step_k += 1
                    n_first = (npull + 1) // 2
                    n_second = npull - n_first
                    for _ in range(n_first):
                        try:
                            next(filler)
                        except StopIteration:
                            break
                    fill_left -= npull
                    steps_left -= 1
                else:
                    n_second = 0
                if 0 <= i - 2 < nsteps:
                    st_T(i - 2)
                if 0 <= i - 3 < nsteps:
                    st_PV(i - 3)
                for _ in range(n_second):
                    try:
                        next(filler)
                    except StopIteration:
                        break
                si_ = i - (nsteps + 3 - len(samp_sched))
                if 0 <= si_ < len(samp_sched):
                    for fn_, bb_ in samp_sched[si_]:
                        fn_(bb_)
            for _ in filler:
                pass

            ckpt(4 + 10 * l)
            for half in range(2):
                bank = next_bank()
                for jj in range(4):
                    j = half * 4 + jj
                    OP("pe", lambda h, j=j, jj=jj, bank=bank: h.transpose(
                        out=ps[bank][0:10, jj * 128:(jj + 1) * 128], in_=ucol[:, j, :], identity=idf[:, :]),
                       r=[b_ucol, b_const], w=[b_ps[bank]], sig=(jj == 3))
                OP("act", lambda h, half=half, bank=bank: h.copy(out=zr[half][0:10, :], in_=ps[bank][0:10, :]),
                   r=[b_ps[bank]], w=[b_zr[half]])
                DMA("sp", lambda h, l=l, half=half: h.dma_start(out=cvpd[l][:, half * 512:(half + 1) * 512], in_=zr[half][0:2, :]),
                    r=[b_zr[half]], w=[b_out])
                DMA("sp", lambda h, l=l, half=half: h.dma_start(out=cvsd[l][:, half * 512:(half + 1) * 512], in_=zr[half][2:10, :]),
                    r=[b_zr[half]], w=[b_out])

            ckpt(5 + 10 * l)

            ckpt(6 + 10 * l)
            pend = []

            def emit_stats(item):
                dch, pi, s, n, zi, qi = item
                OP("pe", lambda h: h.matmul(ps[2 + pi][:, 0:n], lhsT=ones_bf[:, :], rhs=hi[:, dch, s:s + n],
                                            start=(dch == 0), stop=False),
                   r=[b_hip[dch][pi], b_const], w=[b_ps[2 + pi]], sig=False)
                OP("pe", lambda h: h.matmul(ps[2 + pi][:, 0:n], lhsT=ones_bf[:, :], rhs=lo[:, dch, s - 128:s - 128 + n],
                                            start=False, stop=(dch == 15)),
                   r=[b_lop[dch][pi], b_const], w=[b_ps[2 + pi]], sig=(dch == 15))
                OP("pe", lambda h: h.matmul(ps[5 + pi][:, 0:n], lhsT=ones_bf[:, :], rhs=sqb[qi][:, 0:n],
                                            start=(dch == 0), stop=(dch == 15)),
                   r=[b_sqb[qi], b_const], w=[b_ps[5 + pi]], sig=True)

            cnt = 0
            for dch in range(16):
                slot = next_w()
                for pi, (s, n) in enumerate(full):
                    bank = cnt % 2
                    zi = cnt % 3
                    qi = cnt % 3
                    cnt += 1
                    for kc in range(16):
                        OP("pe", lambda h, kc=kc, s=s, n=n, bank=bank, slot=slot: h.matmul(
                            ps[bank][:, 0:n], lhsT=wbuf[slot][:, kc, :], rhs=mixT[:, kc, s - 128:s - 128 + n],
                            start=(kc == 0), stop=(kc == 15)),
                           r=[b_w[slot], b_mix[kc]], w=[b_ps[bank]], sig=(kc == 15))
                    Z = zr[zi]
                    OP("dve", lambda h, Z=Z, dch=dch, s=s, n=n, bank=bank: h.scalar_tensor_tensor(
                        out=Z[:, 0:n], in0=hi[:, dch, s:s + n], scalar=ALPHA, in1=ps[bank][:, 0:n], op0=ALU.mult, op1=ALU.add),
                       r=[b_hi[dch], b_ps[bank]], w=[b_zr[zi]])
                    OP("dve", lambda h, Z=Z, dch=dch, s=s, n=n: h.scalar_tensor_tensor(
                        out=Z[:, 0:n], in0=lo[:, dch, s - 128:s - 128 + n], scalar=ALPHA, in1=Z[:, 0:n], op0=ALU.mult, op1=ALU.add),
                       r=[b_lo[dch]], w=[b_zr[zi]])
                    OP("act", lambda h, Z=Z, dch=dch, s=s, n=n: h.copy(out=hi[:, dch, s:s + n], in_=Z[:, 0:n]),
                       r=[b_zr[zi]], w=[b_hi[dch], b_hip[dch][pi]])
                    OP("act", lambda h, Z=Z, n=n, qi=qi: h.activation(out=sqb[qi][:, 0:n], in_=Z[:, 0:n], func=AF.Square),
                       r=[b_zr[zi]], w=[b_sqb[qi]])
                    OP("dve", lambda h, Z=Z, dch=dch, s=s, n=n: h.tensor_tensor(
                        out=lo[:, dch, s - 128:s - 128 + n], in0=Z[:, 0:n], in1=hi[:, dch, s:s + n], op=ALU.subtract),
                       r=[b_zr[zi], b_hip[dch][pi]], w=[b_lo[dch], b_lop[dch][pi]])
                    pend.append((dch, pi, s, n, zi, qi))
                    if len(pend) > 2:
                        emit_stats(pend.pop(0))
            while pend:
                emit_stats(pend.pop(0))

            ckpt(7 + 10 * l)
            PCS = [(pi, s, n, s - 128, zr[pi]) for pi, (s, n) in enumerate(full)]
            for (pi, s, n, a, T) in PCS:
                OP("dve", lambda h, n=n, pi=pi, T=T: h.tensor_scalar(out=T[:, 0:n], in0=ps[2 + pi][:, 0:n],
                                                                     scalar1=1.0 / D, scalar2=None, op0=ALU.mult),
                   r=[b_ps[2 + pi]], w=[b_zr[pi]])
            for (pi, s, n, a, T) in PCS:
                OP("act", lambda h, a=a, n=n, T=T: h.copy(out=MH[:, a:a + n], in_=T[:, 0:n]), r=[b_zr[pi]], w=[b_scrA])
            for (pi, s, n, a, T) in PCS:
                OP("dve", lambda h, a=a, n=n, T=T: h.tensor_tensor(out=ML[:, a:a + n], in0=T[:, 0:n], in1=MH[:, a:a + n], op=ALU.subtract),
                   r=[b_zr[pi]], w=[b_scrA])
                OP("dve", lambda h, n=n, T=T: h.tensor_tensor(out=T[:, 0:n], in0=T[:, 0:n], in1=T[:, 0:n], op=ALU.mult),
                   r=[], w=[b_zr[pi]])
                OP("dve", lambda h, n=n, T=T, pi=pi: h.scalar_tensor_tensor(
                    out=T[:, 0:n], in0=ps[5 + pi][:, 0:n], scalar=1.0 / D, in1=T[:, 0:n], op0=ALU.mult, op1=ALU.subtract),
                   r=[b_ps[5 + pi]], w=[b_zr[pi]])
            for (pi, s, n, a, T) in PCS:
                OP("act", lambda h, n=n, T=T: h.activation(out=T[:, 0:n], in_=T[:, 0:n], func=AF.Sqrt, bias=epsb[:, 0:1], scale=1.0),
                   r=[b_const], w=[b_zr[pi]])
            for (pi, s, n, a, T) in PCS:
                OP("dve", lambda h, a=a, n=n, T=T: h.reciprocal(out=scrB[:, a:a + n], in_=T[:, 0:n]),
                   r=[b_zr[pi]], w=[b_scrB])

            ckpt(8 + 10 * l)
            opend = []
            ocount = [0]

            def emit_out(item):
                T, zi, s, n, dch = item
                bank = 4 + ocount[0] % 2
                og = ocount[0] % 8
                ocount[0] += 1
                if s < NP:
                    nb = n // 128
                    for j in range(nb):
                        OP("pe", lambda h, j=j: h.transpose(
                            out=ps[bank][:, j * 128:(j + 1) * 128], in_=T[:, j * 128:(j + 1) * 128], identity=idf[:, :]),
                           r=[b_zr[zi], b_const], w=[b_ps[bank]], sig=(j == nb - 1))
                    OP("dve", lambda h: h.tensor_copy(
                        out=ostage8[og][:, 0:nb, :], in_=ps[bank][:, 0:nb * 128].rearrange("p (j f) -> p j f", j=nb)),
                       r=[b_ps[bank]], w=[b_ost8[og]])
                    r0 = s - 512
                    DMA("sp", lambda h: h.dma_start(
                        out=yd[r0:r0 + nb * 128, dch * 128:(dch + 1) * 128].rearrange("(j p) f -> p j f", p=128),
                        in_=ostage8[og][:, 0:nb, :]), r=[b_ost8[og]], w=[b_out])
                else:
                    OP("pe", lambda h: h.transpose(out=ps[bank][0:NS, 0:128], in_=T[:, 0:NS], identity=idf[:, :]),
                       r=[b_zr[zi], b_const], w=[b_ps[bank]])
                    OP("dve", lambda h: h.tensor_copy(out=ostage8[og][0:NS, 0, :], in_=ps[bank][0:NS, 0:128]),
                       r=[b_ps[bank]], w=[b_ost8[og]])
                    DMA("sp", lambda h: h.dma_start(
                        out=ysd[:, dch * 128:(dch + 1) * 128], in_=ostage8[og][0:NS, 0, :]), r=[b_ost8[og]], w=[b_out])

            cnt = 0
            ocnt = 0
            for dch in range(16):
                gcol = l * 16 + dch
                for pi, (s, n) in enumerate(full):
                    a = s - 128
                    zi = cnt % 3
                    cnt += 1
                    T = zr[zi]
                    nb_ = cnt % 4
                    for mi, (lh, rh, bufs) in enumerate((
                            (identb, hi[:, dch, s:s + n], [b_hip[dch][pi]]),
                            (identb, lo[:, dch, s - 128:s - 128 + n], [b_lop[dch][pi]]),
                            (nidentb, MH[:, a:a + n], [b_scrA]),
                            (nidentb, ML[:, a:a + n], [b_scrA]))):
                        OP("pe", lambda h, lh=lh, rh=rh, n=n, nb_=nb_, mi=mi: h.matmul(
                            ps[nb_][:, 0:n], lhsT=lh[:, :], rhs=rh, start=(mi == 0), stop=(mi == 3)),
                           r=bufs + [b_const], w=[b_ps[nb_]], sig=(mi == 3))
                    OP("dve", lambda h, T=T, a=a, n=n, nb_=nb_: h.tensor_tensor(
                        out=T[:, 0:n], in0=ps[nb_][:, 0:n], in1=scrB[:, a:a + n], op=ALU.mult),
                       r=[b_ps[nb_], b_scrB], w=[b_zr[zi]])
                    OP("act", lambda h, T=T, n=n, gcol=gcol: h.activation(
                        out=T[:, 0:n], in_=T[:, 0:n], func=AF.Identity, bias=lnb[:, gcol:gcol + 1], scale=lng[:, gcol:gcol + 1]),
                       r=[b_const], w=[b_zr[zi]])
                    if not last:
                        OP("dve", lambda h, T=T, dch=dch, s=s, n=n: h.tensor_copy(out=hi[:, dch, s:s + n], in_=T[:, 0:n]),
                           r=[b_zr[zi]], w=[b_hi[dch], b_hip[dch][pi]])
                        OP("pool", lambda h, T=T, dch=dch, s=s, n=n: h.tensor_tensor(
                            out=lo[:, dch, s - 128:s - 128 + n], in0=T[:, 0:n], in1=hi[:, dch, s:s + n], op=ALU.subtract),
                           r=[b_zr[zi], b_hip[dch][pi]], w=[b_lo[dch], b_lop[dch][pi]])
                    else:
                        opend.append((T, zi, s, n, dch))
                        if len(opend) > 1:
                            emit_out(opend.pop(0))
            while last and opend:
                emit_out(opend.pop(0))

        _STOPPED[0] = False
        allb = (b_hi + b_lo + b_mix + b_kT + b_v + b_w + b_q + [b_scrA, b_scrB] + b_zr + b_Sp + b_sqb + b_st + b_ps
                + b_xst + b_ost8 + [b_const, b_qs, b_usx, b_ysx, b_ucol, b_out] + b_vnew + b_vsbf2 + b_ksT2 + b_sst2 + b_PTs2)
        fw.final_wait("sp", allb)
        fw.replay(block)
    return nc


def _win_col_order():
    ATT, KV, CONV = 1024, 256, 1024
    q0, k0, v0, ga0 = 0, ATT, ATT + KV, ATT + 2 * KV
    b0 = ga0 + ATT
    cc0 = b0 + CONV
    h0 = cc0 + CONV
    gc0 = h0 + CONV
    cols = []
    cols += list(range(v0, v0 + 256))
    cols += list(range(k0, k0 + 256))
    def G(c):
        a, b = head_of(c, 0), head_of(c, 1)
        return list(range(ga0 + a * 64, ga0 + a * 64 + 64)) + list(range(ga0 + b * 64, ga0 + b * 64 + 64))
    def Q(c):
        a, b = head_of(c, 0), head_of(c, 1)
        return list(range(q0 + a * 64, q0 + a * 64 + 64)) + list(range(q0 + b * 64, q0 + b * 64 + 64))
    cols += G(0) + Q(0)
    for c in range(8):
        j = c
        cols += list(range(cc0 + j * 128, cc0 + (j + 1) * 128))
        cols += list(range(h0 + j * 128, h0 + (j + 1) * 128))
        if c + 1 < 8:
            cols += G(c + 1) + Q(c + 1)
        cols += list(range(b0 + j * 128, b0 + (j + 1) * 128))
        cols += list(range(gc0 + j * 128, gc0 + (j + 1) * 128))
    assert len(cols) == NWIN * 128
    return np.array(cols)


def _mix_row_order():
    rows = []
    for c in range(8):
        a, b = head_of(c, 0), head_of(c, 1)
        rows += list(range(a * 64, a * 64 + 64)) + list(range(b * 64, b * 64 + 64))
    rows += list(range(1024, 2048))
    return np.array(rows)


_CACHE = {}


def kernel(x_prompt, x_sample, cache_k, cache_v, state_conv, meta_tokens,
           w_in, conv_w, sinks, w_out, ln_g, ln_b):
    f32 = np.float32
    x_prompt = np.asarray(x_prompt, f32); x_sample = np.asarray(x_sample, f32)
    cache_k = np.asarray(cache_k, f32); cache_v = np.asarray(cache_v, f32)
    state_conv = np.asarray(state_conv, f32); meta_tokens = np.asarray(meta_tokens, f32)
    w_in = np.asarray(w_in, f32); conv_w = np.asarray(conv_w, f32); sinks = np.asarray(sinks, f32)
    w_out = np.asarray(w_out, f32); ln_g = np.asarray(ln_g, f32); ln_b = np.asarray(ln_b, f32)

    if "nc" not in _CACHE:
        _CACHE["nc"] = build_program()
    nc = _CACHE["nc"]

    corder = _win_col_order()
    win = w_in[:, :, corder].reshape(DEPTH, 16, 128, NWIN, 128).transpose(0, 3, 2, 1, 4)
    win = np.ascontiguousarray(win).reshape(DEPTH * NWIN, 128, 2048)
    rorder = _mix_row_order()
    wout = w_out[:, rorder, :].reshape(DEPTH, 16, 128, 16, 128).transpose(0, 3, 2, 1, 4)
    wout = np.ascontiguousarray(wout).reshape(DEPTH * 16, 128, 2048)
    lng = np.ascontiguousarray(ln_g.reshape(DEPTH, 16, 128).transpose(2, 0, 1).reshape(128, 64))
    lnb = np.ascontiguousarray(ln_b.reshape(DEPTH, 16, 128).transpose(2, 0, 1).reshape(128, 64))
    cw = np.ascontiguousarray(conv_w.reshape(DEPTH, 3, 8, 128).transpose(3, 0, 1, 2).reshape(128, 96))
    sperm = np.array([[head_of(c, e) for c in range(8) for e in range(2)]]).reshape(-1)
    sink_bc = np.ascontiguousarray(np.broadcast_to(sinks[:, sperm].reshape(1, 64), (128, 64)))
    srow = sinks.reshape(DEPTH, 4, 4).transpose(2, 0, 1)
    sink_row = np.ascontiguousarray(np.broadcast_to(srow[:, None], (4, 4, DEPTH, 4)).reshape(16, 16))
    ident = np.eye(128, dtype=f32)
    qi = np.arange(128)[:, None]; si = np.arange(256)[None, :]
    dist = 128 + qi - si
    dgen = np.where((dist >= 0) & (dist < 128), -dist.astype(f32), f32(NEG)).astype(f32)
    ti = np.arange(4)[:, None]; ri = np.arange(132)[None, :]
    sdist = 128 + ti - ri
    svalid = (sdist >= 0) & (sdist < 128)
    dms = np.zeros((16, 4, 132), f32)
    for g in range(4):
        for gi in range(4):
            sl = f32(slope(4 * g + gi))
            dms[gi * 4:(gi + 1) * 4, g, :] = np.where(svalid, -sl * sdist.astype(f32), f32(-3.0e6))
    dms = dms.reshape(16, 4 * 132)

    in_maps = []
    for core in range(8):
        b, j = divmod(core, 4)
        xin = np.zeros((NP, D), f32)
        cm = np.ones((128, 386), f32)
        dmt = np.broadcast_to(dgen[:, None, :], (128, 5, 256)).copy()
        if j == 0:
            xin[496:512] = meta_tokens
            xin[512:] = x_prompt[b, 0:1024]
            cm[:, 0:370] = 0.0
            for bi in range(1, 5):
                kcols = np.arange((bi - 1) * 128, (bi + 1) * 128)
                dmt[:, bi, kcols < 496] = NEG
        else:
            xin[:] = x_prompt[b, j * 1024 - 512:(j + 1) * 1024]
        sb0 = core * 4
        in_maps.append({
            "xin": xin,
            "xs": np.ascontiguousarray(x_sample[sb0:sb0 + 4].reshape(NS, D)),
            "ck": np.ascontiguousarray(cache_k[:, sb0:sb0 + 4].reshape(DEPTH, 4, 128, 256)),
            "cv": np.ascontiguousarray(cache_v[:, sb0:sb0 + 4].reshape(DEPTH, 4, 128, 256)),
            "sc": np.ascontiguousarray(state_conv[:, sb0:sb0 + 4].reshape(DEPTH, 8, 1024)),
            "win": win, "wout": wout, "lng": lng, "lnb": lnb, "cw": cw,
            "sinkbc": sink_bc, "sinkrow": sink_row,
            "dm": np.ascontiguousarray(dmt.reshape(128, 5 * 256)), "dms": dms,
            "cmask": cm, "ident": ident,
        })

    ncores = int(os.environ.get("KCORES", "8"))
    res = run_bass_kernel_spmd(nc, in_maps[:ncores], core_ids=list(range(ncores)))
    R = list(res.results) + [res.results[0]] * (8 - ncores)

    y_prompt = np.zeros((2, 4096, D), f32)
    y_sample = np.zeros((32, 4, D), f32)
    kwp = np.zeros((DEPTH, 2, 128, 4, 64), f32); vwp = np.zeros_like(kwp)
    cvp = np.zeros((DEPTH, 2, 2, 1024), f32)
    kws = np.zeros((DEPTH, 32, 128, 4, 64), f32); vws = np.zeros_like(kws)
    cvs = np.zeros((DEPTH, 32, 2, 1024), f32)
    for core in range(8):
        b, j = divmod(core, 4)
        r = R[core]
        y_prompt[b, j * 1024:(j + 1) * 1024] = r["y"]
        sb0 = core * 4
        y_sample[sb0:sb0 + 4] = r["ys"].reshape(4, 4, D)
        kws[:, sb0:sb0 + 4] = r["kws"].reshape(DEPTH, 4, 128, 4, 64)
        vws[:, sb0:sb0 + 4] = r["vws"].reshape(DEPTH, 4, 128, 4, 64)
        cvs[:, sb0:sb0 + 4] = r["convs"].reshape(DEPTH, 4, 2, 1024)
        if j == 3:
            kwp[:, b] = r["kwp"].reshape(DEPTH, 128, 4, 64)
            vwp[:, b] = r["vwp"].reshape(DEPTH, 128, 4, 64)
            cvp[:, b] = r["convp"]
    return (y_prompt, y_sample, kwp, vwp, cvp, kws, vws, cvs)
```
